# Optimizing a Trainium2 kernel written in Bass

```python
import jax, jax.numpy as jnp
from jax import lax
import numpy as np

D_MODEL = 1024
BATCH = 4
SEQ = 4096
DEPTH = 1

D_POOL = D_MODEL // 2
POOL_WINDOWS = (2, 4, 8, 16)
POOL_GROUPS = len(POOL_WINDOWS)
POOL_GROUP_DIM = D_POOL // POOL_GROUPS
D_GLA_V = D_MODEL // 2
GLA_HEADS = 4
GLA_DV = D_GLA_V // GLA_HEADS
GLA_DK = GLA_DV // 2
D_GLA_K = GLA_HEADS * GLA_DK
GATE_RANK = 16
GATE_TAU = 16.0
CHUNK = 16
D_MIX = D_POOL + D_GLA_V
D_IN = D_POOL + 2 * D_GLA_K + D_GLA_V + GATE_RANK + D_GLA_V
D_FF = 2816
CONV_WIDTH = 3
EPS = 1e-6

kernel_name = "hybrid_pool_gla_convglu_layer"


def rmsnorm(x, g):
    xf = x.astype(jnp.float32)
    y = xf * lax.rsqrt(jnp.mean(xf * xf, axis=-1, keepdims=True) + EPS)
    return (y * g.astype(jnp.float32)).astype(x.dtype)


def multiscale_pool(u, w_pool, pool_scale):
    B, S, _ = u.shape
    uf = u.astype(jnp.float32)
    cs = jnp.cumsum(uf, axis=1)
    pos = jnp.arange(1, S + 1, dtype=jnp.float32)[:, None]
    diffs = []
    for g, w in enumerate(POOL_WINDOWS):
        sl = slice(g * POOL_GROUP_DIM, (g + 1) * POOL_GROUP_DIM)
        c = cs[..., sl]
        lagged = jnp.pad(c, ((0, 0), (w, 0), (0, 0)))[:, :S]
        mean = (c - lagged) / jnp.minimum(pos, float(w))
        diffs.append(mean - uf[..., sl])
    d = jnp.stack(diffs, axis=2)
    y = jnp.einsum('bsgc,gcd->bsgd', d, w_pool.astype(jnp.float32)).reshape(B, S, D_POOL)
    return (y * pool_scale.astype(jnp.float32)).astype(u.dtype)


def gla_chunked(q, k, v, log_a):
    B, S, H, DK = q.shape
    DV = v.shape[-1]
    N = S // CHUNK

    def to_chunks(t):
        return t.astype(jnp.float32).reshape(B, N, CHUNK, H, t.shape[-1]).transpose(0, 3, 1, 2, 4)

    qc = to_chunks(q) * (DK ** -0.5)
    kc = to_chunks(k)
    vc = to_chunks(v)
    b = jnp.cumsum(to_chunks(log_a), axis=3)

    causal = jnp.tril(jnp.ones((CHUNK, CHUNK), dtype=bool))
    rel = b[..., :, None, :] - b[..., None, :, :]
    rel_decay = jnp.exp(jnp.where(causal[:, :, None], rel, -jnp.inf))
    scores = jnp.einsum('bhnid,bhnjd,bhnijd->bhnij', qc, kc, rel_decay)
    o_intra = jnp.einsum('bhnij,bhnjv->bhniv', scores, vc)

    b_last = b[..., -1, :]
    kv = jnp.einsum('bhncd,bhncv->bhndv', kc * jnp.exp(b_last[..., None, :] - b), vc)
    chunk_decay = jnp.exp(b_last)

    def step(state, inp):
        dec, kv_n = inp
        return dec[..., None] * state + kv_n, state

    state0 = jnp.zeros((B, H, DK, DV), jnp.float32)
    _, s_prev = lax.scan(step, state0, (jnp.moveaxis(chunk_decay, 2, 0), jnp.moveaxis(kv, 2, 0)))
    s_prev = jnp.moveaxis(s_prev, 0, 2)
    o_inter = jnp.einsum('bhncd,bhndv->bhncv', qc * jnp.exp(b), s_prev)

    o = (o_intra + o_inter).transpose(0, 2, 3, 1, 4).reshape(B, S, H, DV)
    return o.astype(q.dtype)


def token_mixer(h, w_in, w_pool, pool_scale, w_gate_up, b_gate, g_gla_norm, w_out):
    B, S, _ = h.shape
    proj = h @ w_in
    cuts = [int(c) for c in np.cumsum([D_POOL, D_GLA_K, D_GLA_K, D_GLA_V, GATE_RANK])]
    u_pool, q, k, v, g_low, r = jnp.split(proj, cuts, axis=-1)

    y_pool = multiscale_pool(u_pool, w_pool, pool_scale)

    gate_logits = (g_low @ w_gate_up + b_gate).astype(jnp.float32)
    log_a = jax.nn.log_sigmoid(gate_logits) / GATE_TAU
    o = gla_chunked(q.reshape(B, S, GLA_HEADS, GLA_DK),
                    k.reshape(B, S, GLA_HEADS, GLA_DK),
                    v.reshape(B, S, GLA_HEADS, GLA_DV),
                    log_a.reshape(B, S, GLA_HEADS, GLA_DK))
    o = rmsnorm(o, g_gla_norm) * jax.nn.silu(r.reshape(B, S, GLA_HEADS, GLA_DV))
    y_gla = o.reshape(B, S, D_GLA_V)

    y = jnp.concatenate([y_pool, y_gla.astype(y_pool.dtype)], axis=-1)
    return y @ w_out


def conv_glu_ffn(h, w_ffn_in, conv_w, conv_b, w_ffn_out):
    up = h @ w_ffn_in
    gate, val = jnp.split(up, 2, axis=-1)
    gate = lax.conv_general_dilated(
        gate, conv_w[:, None, :].astype(gate.dtype), window_strides=(1,),
        padding=[(CONV_WIDTH - 1, 0)], dimension_numbers=('NWC', 'WIO', 'NWC'),
        feature_group_count=D_FF) + conv_b
    return (jax.nn.gelu(gate, approximate=False) * val) @ w_ffn_out


def setup_inputs(seed: int = 0) -> dict:
    key = jax.random.key(seed)
    ks = jax.random.split(key, 16)
    f32 = jnp.float32
    nrm = lambda k, shape, scale: jax.random.normal(k, shape, f32) * scale
    return {
        "x": nrm(ks[0], (BATCH, SEQ, D_MODEL), 1.0),
        "g_pre_mix": 1.0 + nrm(ks[1], (D_MODEL,), 0.05),
        "w_in": nrm(ks[2], (D_MODEL, D_IN), D_MODEL ** -0.5),
        "w_pool": nrm(ks[3], (POOL_GROUPS, POOL_GROUP_DIM, POOL_GROUP_DIM), POOL_GROUP_DIM ** -0.5),
        "pool_scale": 1.0 + nrm(ks[4], (D_POOL,), 0.05),
        "w_gate_up": nrm(ks[5], (GATE_RANK, D_GLA_K), GATE_RANK ** -0.5),
        "b_gate": nrm(ks[6], (D_GLA_K,), 0.1),
        "g_gla_norm": 1.0 + nrm(ks[7], (GLA_DV,), 0.05),
        "w_out": nrm(ks[8], (D_MIX, D_MODEL), D_MIX ** -0.5),
        "g_post_mix": 1.0 + nrm(ks[9], (D_MODEL,), 0.05),
        "g_pre_ffn": 1.0 + nrm(ks[10], (D_MODEL,), 0.05),
        "w_ffn_in": nrm(ks[11], (D_MODEL, 2 * D_FF), D_MODEL ** -0.5),
        "conv_w": nrm(ks[12], (CONV_WIDTH, D_FF), CONV_WIDTH ** -0.5),
        "conv_b": nrm(ks[13], (D_FF,), 0.02),
        "w_ffn_out": nrm(ks[14], (D_FF, D_MODEL), D_FF ** -0.5),
        "g_post_ffn": 1.0 + nrm(ks[15], (D_MODEL,), 0.05),
    }


def reference(x, g_pre_mix, w_in, w_pool, pool_scale, w_gate_up, b_gate, g_gla_norm, w_out,
              g_post_mix, g_pre_ffn, w_ffn_in, conv_w, conv_b, w_ffn_out, g_post_ffn):
    for _ in range(DEPTH):
        h = rmsnorm(x, g_pre_mix)
        mix = token_mixer(h, w_in, w_pool, pool_scale, w_gate_up, b_gate, g_gla_norm, w_out)
        x = x + rmsnorm(mix, g_post_mix)
        h = rmsnorm(x, g_pre_ffn)
        ff = conv_glu_ffn(h, w_ffn_in, conv_w, conv_b, w_ffn_out)
        x = x + rmsnorm(ff, g_post_ffn)
    return x
```

```python
import numpy as np
from contextlib import ExitStack
import concourse.bass as bass
import concourse.mybir as mybir
from concourse.bass_utils import run_bass_kernel_spmd

F32 = mybir.dt.float32
BF16 = mybir.dt.bfloat16
U8 = mybir.dt.uint8
AF = mybir.ActivationFunctionType
ALU = mybir.AluOpType

P = 128
D = 1024
KD = 8
DIN = 2064
DFF = 2816
NF = 22
NT_ALL = 32
HALO_T = 15
EPS = 1e-6
C_U, C_Q, C_K, C_V, C_G, C_R = 0, 512, 768, 1024, 1536, 1552
FBLOCKS = [(0, 2), (2, 4), (6, 4), (10, 4), (14, 4), (18, 4)]
NCOL = 2 + 1 + 4 + 66 + 22


class _Op:
    __slots__ = ("eng", "emit", "deps", "lane", "idx", "eidx", "signal", "sigcount", "lanecount", "waits", "fuse")


class Sched:
    ENG = ("pe", "act", "dve", "pool", "sp")

    def __init__(self):
        self.ops = []
        self.byeng = {e: [] for e in self.ENG}
        self.lastw = {}
        self.readers = {}
        self.lane_last = {}
        self.lane_cnt = {}
        self.dom_last = {}
        self.tag = None
        self.tag_last = {}
        self.group = None

    def fence_tags(self, tags):
        out = set()
        for t in tags:
            out.update(self.tag_last.get(t, {}).values())
        return out

    def _dom(self, op):
        return ("L", op.lane) if op.lane is not None else op.eng

    def add(self, eng, emit, reads=(), writes=(), lane=None, extra=(), cost=0.3, fuse=None):
        idx = len(self.ops)
        deps = set(extra)
        for r in reads:
            if r in self.lastw:
                deps.add(self.lastw[r])
        for w in writes:
            if w in self.lastw:
                deps.add(self.lastw[w])
            deps.update(self.readers.get(w, {}).values())
        if lane is not None and lane in self.lane_last:
            deps.add(self.lane_last[lane])
        deps.discard(idx)
        op = _Op()
        op.eng, op.emit, op.deps, op.lane, op.idx = eng, emit, deps, lane, idx
        op.signal, op.sigcount, op.lanecount, op.waits = False, 0, 0, None
        op.fuse = (eng in ("dve", "act") and lane is None) if fuse is None else fuse
        dom = ("L", lane) if lane is not None else eng
        for r in reads:
            self.readers.setdefault(r, {})[dom] = idx
        for w in writes:
            self.lastw[w] = idx
            self.readers[w] = {}
        if lane is not None:
            self.lane_last[lane] = idx
            self.lane_cnt[lane] = self.lane_cnt.get(lane, 0) + 16
            op.lanecount = self.lane_cnt[lane]
        op.eidx = len(self.byeng[eng])
        self.ops.append(op)
        self.byeng[eng].append(op)
        self.dom_last[dom] = idx
        self.tag_last.setdefault(self.tag, {})[dom] = idx
        if self.group is not None:
            self.group.append((idx, eng, cost, tuple(deps), lane is not None))
        return idx

    def fence(self):
        return set(self.dom_last.values())

    def _needed(self, x, d):
        if d.lane is not None:
            return True
        if d.eng == x.eng and x.lane is None:
            return x.eng != "pe"
        return True

    def finalize(self):
        ops = self.ops

        def dom(o):
            return ("L", o.lane) if o.lane is not None else o.eng

        know = {e: {} for e in self.ENG}
        after = [None] * len(ops)
        needed = [None] * len(ops)
        for x in ops:
            E = x.eng
            K = dict(know[E])
            nd = []
            for di in sorted(x.deps, reverse=True):
                d = ops[di]
                if d.lane is None and x.lane is None and d.eng == "pe" and E == "pe":
                    continue
                dm = dom(d)
                if K.get(dm, -1) >= di:
                    continue
                nd.append(di)
                for k2, v2 in after[di].items():
                    if K.get(k2, -1) < v2:
                        K[k2] = v2
            needed[x.idx] = nd
            know[E] = K
            a_ = dict(K)
            dx = dom(x)
            if a_.get(dx, -1) < x.idx:
                a_[dx] = x.idx
            after[x.idx] = a_
        for x in ops:
            for di in needed[x.idx]:
                if ops[di].lane is None:
                    ops[di].signal = True
        for e in self.ENG:
            c = 0
            for op in self.byeng[e]:
                if op.lane is None and op.signal:
                    c += 1
                op.sigcount = c
        for x in ops:
            w = {}
            for di in needed[x.idx]:
                d = ops[di]
                if d.lane is not None:
                    key, val = ("L", d.lane), d.lanecount
                else:
                    key, val = d.eng, d.sigcount
                if w.get(key, 0) < val:
                    w[key] = val
            x.waits = w

    def emit_engine(self, e, engine, sems):
        for x in self.byeng[e]:
            w = list(x.waits.items())
            fuse_ok = bool(x.fuse)
            if x.fuse == "w" and any(isinstance(k, tuple) for k, _ in w):
                fuse_ok = False
            fused = w.pop() if (w and fuse_ok) else None
            for k, v in w:
                engine.wait_ge(sems[k], v)
            ins = x.emit(engine)
            if fused is not None:
                ins._wait_ge(sems[fused[0]], fused[1])
            if x.lane is not None:
                ins.then_inc(sems[("L", x.lane)], 16)
            elif x.signal:
                ins.then_inc(sems[e], 1)


def build_program(stop_after_mixer=False, plan=None, record=None, opts=()):
    nc = bass.Bass("TRN2", target_bir_lowering=False)

    def din(name, shape, dt=F32):
        return nc.dram_tensor(name, list(shape), dt, kind="ExternalInput").ap()

    xall = din("xall", [NT_ALL * P, D])
    w_in = din("w_in", [D, DIN])
    w_out = din("w_out", [D, D])
    w_fi = din("w_ffn_in", [D, 2 * DFF])
    w_fo = din("w_ffn_out", [DFF, D])
    w_pool = din("w_pool", [4, P, P])
    w_gu = din("w_gu", [16, 256])
    gbs = din("gbs", [4, P, D])
    cols_d = din("cols", [P, NCOL])
    cmask = din("cmask", [P, 3, P])
    pmats = din("pmats", [3, 4, P, P])
    psb_d = din("psb", [P, 4, P])
    out = nc.dram_tensor("out", [16 * P, D], F32, kind="ExternalOutput").ap()

    S = Sched()
    TOTAL = 212480
    es = ExitStack()
    big = es.enter_context(nc.sbuf_tensor("big", [P, TOTAL], U8))
    psum = es.enter_context(nc.psum_tensor("ps", [P, 8, 512], F32))

    def carve(off, shape, dt):
        n = int(np.prod(shape[1:]))
        sz = n * (4 if dt == F32 else 2)
        ap = big[0:shape[0], off:off + sz].bitcast(dt)
        if len(shape) == 3:
            ap = ap.rearrange("p (a b) -> p a b", b=shape[2])
        elif len(shape) == 4:
            ap = ap.rearrange("p (a b c) -> p a b c", b=shape[2], c=shape[3])
        return ap

    class Arena:
        def __init__(self, base, limit):
            self.off, self.limit = base, limit

        def get(self, shape, dt):
            n = int(np.prod(shape[1:])) * (4 if dt == F32 else 2)
            n = (n + 63) // 64 * 64
            o = self.off
            self.off += n
            assert self.off <= self.limit, (self.off, self.limit)
            return carve(o, shape, dt)

    PERS = 10240
    pa = Arena(0, PERS)
    ident = pa.get([P, P], BF16)
    ones_bf = pa.get([P, P], BF16)
    gb_pre_ffn = pa.get([P, D], F32)
    gb_post_ffn = pa.get([P, D], F32)
    cols = pa.get([P, NCOL], F32)
    negb = pa.get([P, 2], F32)
    hal = [pa.get([P, NF, 2], F32) for _ in range(2)]
    hh = pa.get([P, KD, 2], BF16)
    stat = pa.get([P, 64], F32)
    mhalf = pa.get([P, 1], F32)
    c_bg = cols[:, 0:2]
    c_ggla = cols[:, 2:3]
    c_psc = cols[:, 3:7]
    c_cw = cols[:, 7:73].rearrange("p (f t) -> p f t", t=3)
    c_cb = cols[:, 73:95]

    FW0 = PERS
    fblk = []
    off = FW0
    gv_off = []
    for (f0, nf) in FBLOCKS:
        gv_off.append(off)
        off += 2 * KD * nf * P * 2
    wo_off = []
    for (f0, nf) in FBLOCKS:
        wo_off.append(off)
        off += nf * D * 2
    for bi, (f0, nf) in enumerate(FBLOCKS):
        g = carve(gv_off[bi], [P, KD, nf * P], BF16)
        v = carve(gv_off[bi] + KD * nf * P * 2, [P, KD, nf * P], BF16)
        o = carve(wo_off[bi], [P, nf, D], BF16)
        fblk.append((f0, nf, g, v, o, gv_off[bi]))
    FW_END = off
    assert FW_END == PERS + 135168

    M_SIZE = 193920
    M_BASE = TOTAL - M_SIZE
    ma = Arena(M_BASE, TOTAL)
    w_in_sb = ma.get([P, KD, DIN], BF16)
    hT2 = [ma.get([P, KD, 512], BF16) for _ in range(2)]
    m_xa = [ma.get([P, D], F32) for _ in range(3)]
    setmp = [ma.get([P, 512], F32) for _ in range(2)]
    m_xs = [ma.get([P, D], BF16) for _ in range(2)]
    glT = ma.get([16, 512], BF16)
    lg = ma.get([P, 2, 512], F32)
    braw = ma.get([P, 2, 512], F32)
    Ek = ma.get([P, 2, 512], F32)
    w_gu_sb = ma.get([16, 256], BF16)
    gb_pre_mix = ma.get([P, D], F32)
    ones_f = ma.get([P, P], F32)
    A_END = ma.off
    w_out_sb = ma.get([P, KD, D], BF16)
    w_pool_sb = ma.get([P, 4, P], BF16)
    gb_post_mix = ma.get([P, D], F32)
    trim = ma.get([P, 4, P], BF16)
    pm_sb = ma.get([P, 3, 4 * P], BF16)
    psb = ma.get([P, 4, P], F32)
    m_x1 = [ma.get([P, D], F32) for _ in range(2)]
    hx = ma.get([P, D], BF16)
    _shapes = (("ut", [P, 5, 512], BF16), ("Vt", [P, 4, 512], BF16), ("Qz", [P, 4, 512], BF16),
               ("Kt", [P, 2, 512], BF16), ("Ktt", [P, 4, 256], BF16), ("sr", [P, 4, 512], BF16), ("dec", [P, 2, 4], F32))
    _p0 = {n: ma.get(s, d) for n, s, d in _shapes}
    ATb = [ma.get([P, 4, P], BF16) for _ in range(2)]
    m_x1l = None
    sq = ma.get([P, 4, P], BF16)
    rb = ma.get([P, 4, P], F32)
    tb = ma.get([P, 4, P], F32)
    dT = ma.get([P, 4, P], BF16)
    yT = ma.get([P, KD, 512], BF16)
    m_t1 = ma.get([P, D], F32)
    Sst = [ma.get([P, 2, P], F32) for _ in range(2)]
    Sd = ma.get([P, 2, P], F32)
    Sb = [ma.get([P, 2, P], BF16) for _ in range(2)]
    P1_BASE = ma.off
    _p1 = {n: ma.get(s, d) for n, s, d in _shapes}
    P1_END = ma.off
    ut, Vt, Qz, Kt, Ktt, sr, dec = ([_p0[n], _p1[n]] for n in ("ut", "Vt", "Qz", "Kt", "Ktt", "sr", "dec"))
    def _cls(bend):
        return 0 if bend <= M_BASE else (1 if bend <= A_END else 2)
    gv_class = [_cls(gv_off[bi] + 2 * KD * nf * P * 2) for bi, (f0, nf) in enumerate(FBLOCKS)]
    wo_class = [_cls(wo_off[bi] + nf * D * 2) for bi, (f0, nf) in enumerate(FBLOCKS)]
    M_USED = ma.off

    fa = Arena(FW_END, P1_BASE)
    fp1 = Arena(P1_BASE, P1_END)
    actb = fa.get([P, NF, 512], BF16)
    f_xa = [fp1.get([P, D], F32) for _ in range(2)]
    f_x1 = [fa.get([P, D], F32) for _ in range(2)]
    f_xs = [fp1.get([P, D], BF16) for _ in range(2)]
    h2T = fp1.get([P, KD, 512], BF16)
    accb = [fa.get([P, 512], F32) for _ in range(2)]
    glb = [fa.get([P, 512], F32) for _ in range(2)]
    f_t1 = fa.get([P, D], F32)

    psT = psum[:, 7, :].bitcast(BF16)
    bank_rr = [0]
    held = set()

    def bank():
        for _ in range(8):
            b = bank_rr[0] % 7
            bank_rr[0] += 1
            if b not in held:
                held.add(b)
                return b
        raise AssertionError("no free PSUM bank: %r" % (held,))

    def bank_pair():
        for _ in range(16):
            b = bank_rr[0] % 7
            if b < 6 and b not in held and (b + 1) not in held:
                bank_rr[0] += 2
                held.add(b)
                held.add(b + 1)
                return b
            bank_rr[0] += 1
        raise AssertionError("no free PSUM bank pair: %r" % (held,))

    def rel(*bs):
        for b in bs:
            held.discard(b)

    def PS(b):
        return "ps%d" % b

    stat_rr = [0]

    def statcol(n=1):
        c = stat_rr[0] % 56
        if c + n > 56:
            c = 0
            stat_rr[0] = 0
        stat_rr[0] += n
        return stat[:, c:c + n], "stat%d" % c

    def _n(ap):
        return float(ap.free_size())

    def dma(eng, lane, out_ap, in_ap, reads, writes, extra=()):
        return S.add(eng, lambda e: e.dma_start(out=out_ap, in_=in_ap), reads, writes, lane=lane, extra=extra, cost=3.0)

    def act(out_ap, in_ap, func, reads, writes, bias=None, scale=None, accum=None, extra=()):
        kw = {}
        if bias is not None:
            kw["bias"] = bias
        if scale is not None:
            kw["scale"] = scale
        if accum is not None:
            kw["accum_out"] = accum
        return S.add("act", lambda e: e.activation(out=out_ap, in_=in_ap, func=func, **kw), reads, writes, extra=extra,
                     cost=0.25 + _n(in_ap) / 1200.0 + (0.1 if accum is not None else 0.0),
                     fuse=(accum is None))

    def tt(out_ap, a, b, op, reads, writes, extra=()):
        return S.add("dve", lambda e: e.tensor_tensor(out=out_ap, in0=a, in1=b, op=op), reads, writes, extra=extra,
                     cost=0.12 + _n(a) / 830.0)

    def ts(out_ap, a, s1, s2, op0, op1, reads, writes, extra=()):
        return S.add("dve", lambda e: e.tensor_scalar(out=out_ap, in0=a, scalar1=s1, scalar2=s2, op0=op0, op1=op1),
                     reads, writes, extra=extra, cost=0.12 + _n(a) / 900.0)

    def stt(out_ap, a, sc, b, op0, op1, reads, writes, extra=()):
        return S.add("dve", lambda e: e.scalar_tensor_tensor(out=out_ap, in0=a, scalar=sc, in1=b, op0=op0, op1=op1),
                     reads, writes, extra=extra, cost=0.12 + _n(a) / 830.0)

    def mm(out_ap, lhsT, rhs, start, stop, reads, writes, extra=(), wfuse=False):
        return S.add("pe", lambda e: e.matmul(out_ap, lhsT, rhs, start=start, stop=stop), reads, writes, extra=extra,
                     cost=0.04 + 0.00039 * _n(rhs), fuse=("w" if wfuse else False))

    def tr(out_ap, in_ap, reads, writes, extra=()):
        return S.add("pe", lambda e: e.transpose(out_ap, in_ap, ident), reads + ["ident"], writes, extra=extra, cost=0.11)

    def rstd_pool(ss_ap, ss_res, scale, extra=()):
        tmp, tres = statcol()
        S.add("pool", lambda e: e.tensor_scalar(out=tmp, in0=ss_ap, scalar1=scale, scalar2=EPS, op0=ALU.mult, op1=ALU.add),
              [ss_res], [tres], extra=extra)
        r, rres = statcol()
        S.add("pool", lambda e: e.tensor_tensor(out=r, in0=tmp, in1=mhalf, op=ALU.pow), [tres, "mhalf"], [rres], extra=extra)
        return r, rres

    def rstd_from_ss(ss_ap, ss_res, scale, extra=()):
        tmp, tres = statcol()
        act(tmp, ss_ap, AF.Ln, [ss_res], [tres], bias=EPS, scale=scale, extra=extra)
        r, rres = statcol()
        act(r, tmp, AF.Exp, [tres], [rres], scale=-0.5, extra=extra)
        return r, rres

    lane_id = [0]

    def newlane(prefix):
        lane_id[0] += 1
        return "%s%d" % (prefix, lane_id[0])

    LC = newlane("c")
    dma("sp", newlane("c"), cols, cols_d, [], ["cols"])
    dma("sp", newlane("c"), gb_pre_mix, gbs[0], [], ["gb_pre_mix"])
    wl = [newlane("w") for _ in range(8)]
    wl_i = [0]

    def wdma(out_ap, in_ap, writes, extra=()):
        lane = wl[wl_i[0] % len(wl)]
        wl_i[0] += 1
        return dma("pool", lane, out_ap, in_ap, [], writes, extra=extra)

    wdma(ident, cmask[:, 0, :], ["ident"])
    wdma(w_gu_sb, w_gu, ["w_gu"])
    for h_ in range(4):
        wdma(trim[:, h_, :], cmask[:, 1, :], ["trim%d" % h_])
    wdma(ones_bf, cmask[:, 2, :], ["ones_bf"])
    w_in_v = w_in.rearrange("(k p) n -> p k n", p=P)
    for (c0, c1, nm) in ((C_K, C_R, "w_in_a"), (C_R, DIN, "w_in_c"), (0, C_K, "w_in_b")):
        for k2 in range(0, KD, 2):
            wdma(w_in_sb[:, k2:k2 + 2, c0:c1], w_in_v[:, k2:k2 + 2, c0:c1], ["%s%d" % (nm, k2)])
    W_IN_A = ["w_in_a%d" % k for k in range(0, KD, 2)]
    W_IN_B = ["w_in_b%d" % k for k in range(0, KD, 2)]
    W_IN_C = ["w_in_c%d" % k for k in range(0, KD, 2)]
    S.add("dve", lambda e: e.memset(ones_f, 1.0), [], ["ones_f"])
    S.add("dve", lambda e: e.memset(mhalf, -0.5), [], ["mhalf"])
    ts(negb, c_bg, -1.0, None, ALU.mult, ALU.bypass, ["cols"], ["negb"])
    for g in range(3):
        wdma(pm_sb[:, g, :].rearrange("p (a b) -> p a b", b=P), pmats[g].rearrange("g j i -> j g i"), ["pm%d" % g])
    wdma(w_pool_sb, w_pool.rearrange("g c d -> c g d"), ["w_pool"])
    w_out_v = w_out.rearrange("(k p) n -> p k n", p=P)
    for k2 in range(0, KD, 4):
        wdma(w_out_sb[:, k2:k2 + 4, :], w_out_v[:, k2:k2 + 4, :], ["w_out%d" % k2])
    W_OUT = ["w_out0", "w_out4"]

    w_fi_v = w_fi.rearrange("(k p) n -> p k n", p=P)

    def load_gv(bi, extra=()):
        f0, nf, g, v, o, _ = fblk[bi]
        for k2 in range(0, KD, 4):
            wdma(g[:, k2:k2 + 4, :], w_fi_v[:, k2:k2 + 4, f0 * P:(f0 + nf) * P], ["fwg%d_%d" % (bi, k2)], extra=extra)
            wdma(v[:, k2:k2 + 4, :], w_fi_v[:, k2:k2 + 4, DFF + f0 * P:DFF + (f0 + nf) * P],
                 ["fwv%d_%d" % (bi, k2)], extra=extra)

    def load_wo(bi, extra=()):
        f0, nf, g, v, o, _ = fblk[bi]
        wdma(o, w_fo[f0 * P:(f0 + nf) * P, :].rearrange("(f p) n -> p f n", p=P), ["fwo%d" % bi], extra=extra)

    def load_class(cls, extra=()):
        for bi in range(len(fblk)):
            if gv_class[bi] == cls:
                load_gv(bi, extra=extra)
        for bi in range(len(fblk)):
            if wo_class[bi] == cls:
                load_wo(bi, extra=extra)

    def FWG(bi):
        return ["fwg%d_0" % bi, "fwg%d_4" % bi]

    def FWV(bi):
        return ["fwv%d_0" % bi, "fwv%d_4" % bi]

    xl = [newlane("x") for _ in range(4)]
    xlf = [newlane("xf") for _ in range(2)]
    x1l = [newlane("y") for _ in range(2)]
    prep_cnt = [0]

    def prep_tile_gen(src_ap, xa_bufs, xs_bufs, gb, gbres, dstT, dst_res, col0, tag, extra=(), rstd_fn=None,
                      do_load=True):
        s = prep_cnt[0] % 2
        prep_cnt[0] += 1
        xa, xs = xa_bufs[s], xs_bufs[s]
        ra, rs = "%sxa%d" % (tag, s), "%sxs%d" % (tag, s)
        if do_load:
            dma("sp", xlf[s], xa, src_ap, [], [ra], extra=extra)
        ss, ssr = statcol()
        act(xs, xa, AF.Square, [ra], [rs, ssr], accum=ss, extra=extra)
        r, rres = (rstd_fn or rstd_pool)(ss, ssr, 1.0 / D, extra=extra)
        yield
        stt(xs, xa, r, gb, ALU.mult, ALU.mult, [ra, rres, gbres], [rs], extra=extra)
        yield
        for k in range(KD):
            tr(psT[:, k * P:(k + 1) * P], xs[:, k * P:(k + 1) * P], [rs], ["psT"], extra=extra)
        act(dstT[:, :, col0:col0 + P], psT.rearrange("p (k t) -> p k t", t=P), AF.Copy, ["psT"], [dst_res], extra=extra)
        yield

    def prep_tile(*a, **kw):
        for _ in prep_tile_gen(*a, **kw):
            pass

    def m_load(t):
        s = t % 3
        dma("sp", xl[s], m_xa[s], xall[t * P:(t + 1) * P, :], [], ["mxa%d" % s])

    def m_norm(t, hp, col0, evac_dve=False):
        s = t % 3
        xa, ra = m_xa[s], "mxa%d" % s
        xs, rs = m_xs[t % 2], "mxs%d" % (t % 2)
        ss, ssr = statcol()
        act(xs, xa, AF.Square, [ra], [rs, ssr], accum=ss)
        r, rres = rstd_from_ss(ss, ssr, 1.0 / D)
        yield
        stt(xs, xa, r, gb_pre_mix, ALU.mult, ALU.mult, [ra, rres, "gb_pre_mix"], [rs])
        if t + 3 < NT_ALL:
            m_load(t + 3)
        yield
        for k in range(KD):
            tr(psT[:, k * P:(k + 1) * P], xs[:, k * P:(k + 1) * P], [rs], ["psT"])
        if evac_dve:
            S.add("dve", lambda e: e.tensor_copy(out=hT2[hp][:, :, col0:col0 + P],
                                                   in_=psT.rearrange("p (k t) -> p k t", t=P)),
                  ["psT"], ["hT%d" % hp], cost=0.12 + 1024 / 900.0)
        else:
            act(hT2[hp][:, :, col0:col0 + P], psT.rearrange("p (k t) -> p k t", t=P), AF.Copy, ["psT"], ["hT%d" % hp])
        yield

    state = {"cur": 0, "first": True}

    def make_macro(tiles, full, par):
        nt = len(tiles)
        T = nt * P
        hT = hT2[par]
        RH = "hT%d" % par
        Vt_, ut_, Qz_, Kt_, Ktt_, sr_, dec_ = Vt[par], ut[par], Qz[par], Kt[par], Ktt[par], sr[par], dec[par]
        sx = "_%d" % par
        RV = lambda i: "Vt%d%s" % (i, sx)
        RU = lambda i: "ut%d%s" % (i, sx)
        RQ = lambda h: "Qz%d%s" % (h, sx)
        RK = lambda c: "Kt%d%s" % (c, sx)
        RKT = lambda i: "Ktt%d%s" % (i, sx)
        RS = lambda h: "sr%d%s" % (h, sx)
        RD = "dec" + sx

        def prep_gen():
            gens = [m_norm(tiles[i], par, i * P, evac_dve=not full) for i in range(nt)]
            live = []
            nx = 0
            while nx < nt or live:
                if nx < nt and len(live) < 2:
                    live.append(gens[nx])
                    nx += 1
                for g_ in list(live):
                    try:
                        next(g_)
                    except StopIteration:
                        live.remove(g_)
                yield

        def proj_fm(c0, ncols, wres):
            b = bank()
            for k in range(KD):
                mm(psum[0:ncols, b, 0:T], w_in_sb[:, k, c0:c0 + ncols], hT[:, k, 0:T], k == 0, k == KD - 1,
                   [RH] + wres, [PS(b)], wfuse=True)
            return b

        def vproj(i):
            b = bank()
            for k in range(KD):
                mm(psum[:, b, :], hT[:, k, i * P:(i + 1) * P], w_in_sb[:, k, C_V:C_V + 512], k == 0, k == KD - 1,
                   [RH] + W_IN_A, [PS(b)])
            act(Vt_[:, i, :], psum[:, b, :], AF.Copy, [PS(b)], [RV(i)])
            rel(b)

        def a2a_gen():
            bg = proj_fm(C_G, P, W_IN_A + W_IN_C)
            act(glT[:, 0:T], psum[0:16, bg, 0:T], AF.Copy, [PS(bg)], ["glT"])
            rel(bg)
            yield
            for c in range(2):
                b = bank()
                S.add("pe", lambda e, b=b, c=c: e.matmul(psum[:, b, 0:T], w_gu_sb[:, c * P:(c + 1) * P], glT[:, 0:T],
                                                           start=True, stop=True),
                      ["glT", "w_gu"], [PS(b)])
                act(lg[:, c, 0:T], psum[:, b, 0:T], AF.Exp, [PS(b), "negb"], ["lg%d" % c], bias=negb[:, c:c + 1],
                    scale=-1.0)
                rel(b)
                act(lg[:, c, 0:T], lg[:, c, 0:T], AF.Ln, ["lg%d" % c], ["lg%d" % c], bias=1.0, scale=1.0)
                yield
                for i in range(nt):
                    S.add("dve", lambda e, c=c, i=i: e.tensor_tensor_scan(
                        out=braw[:, c, i * P:(i + 1) * P], data0=ones_f[:, :], data1=lg[:, c, i * P:(i + 1) * P],
                        initial=0.0, op0=ALU.mult, op1=ALU.add), ["lg%d" % c, "ones_f"], ["braw%d" % c])
                yield
                act(Ek[:, c, 0:T], braw[:, c, 0:T], AF.Exp, ["braw%d" % c], ["Ek%d" % c], scale=1.0 / 16.0)
                if full:
                    act(lg[:, c, 0:T], braw[:, c, 0:T], AF.Exp, ["braw%d" % c], ["lg%d" % c],
                        bias=float(np.log(0.125)), scale=-1.0 / 16.0)
                yield
            act(dec_[:, :, 0:nt], braw[:, :, 0:T].rearrange("p c (i t) -> p c i t", t=P)[:, :, :, P - 1], AF.Exp,
                ["braw0", "braw1"], [RD], scale=-1.0 / 16.0)
            for c in range(2):
                bk = proj_fm(C_K + c * P, P, W_IN_A)
                tt(Kt_[:, c, 0:T], psum[:, bk, 0:T], Ek[:, c, 0:T], ALU.mult, [PS(bk), "Ek%d" % c], [RK(c)])
                rel(bk)
                yield
                if full:
                    bq = proj_fm(C_Q + c * P, P, W_IN_B)
                    for e_ in range(2):
                        rows = slice(e_ * 64, (e_ + 1) * 64)
                        tt(Qz_[rows, 2 * c + e_, 0:T], psum[rows, bq, 0:T], lg[rows, c, 0:T], ALU.mult,
                           [PS(bq), "lg%d" % c], [RQ(2 * c + e_)])
                    rel(bq)
                    yield
            for i in range(nt):
                for c in range(2):
                    tr(psT[:, c * P:(c + 1) * P], Kt_[:, c, i * P:(i + 1) * P], [RK(c)], ["psT"])
                act(Ktt_[:, i, :], psT[:, 0:2 * P], AF.Copy, ["psT"], [RKT(i)])
                yield

        def a2b_gen():
            for i in range(nt):
                vproj(i)
                yield
            if full:
                for i in range(nt):
                    b = bank()
                    for k in range(KD):
                        mm(psum[:, b, :], hT[:, k, i * P:(i + 1) * P], w_in_sb[:, k, C_U:C_U + 512], k == 0, k == KD - 1,
                           [RH] + W_IN_B, [PS(b)])
                    act(ut_[:, 1 + i, :], psum[:, b, :], AF.Copy, [PS(b)], [RU(1 + i)])
                    rel(b)
                    yield
                for h in range(4):
                    b = proj_fm(C_R + h * P, P, W_IN_C)
                    se, ser = setmp[h % 2][:, 0:T], "se%d" % (h % 2)
                    act(se, psum[:, b, 0:T], AF.Exp, [PS(b)], [ser], scale=-1.0)
                    act(se, se, AF.Ln, [ser], [ser], bias=1.0, scale=1.0)
                    act(se, se, AF.Exp, [ser], [ser], scale=-1.0)
                    tt(sr_[:, h, 0:T], psum[:, b, 0:T], se, ALU.mult, [PS(b), ser], [RS(h)])
                    rel(b)
                    yield

        def chunk(i):
            t = tiles[i]
            seg = slice(i * P, (i + 1) * P)
            cur = state["cur"]
            nxt = 1 - cur
            first = state["first"]
            if full:
                s1 = (t - HALO_T) % 2
                x1b, x1r = m_x1[s1], "mx1%d" % s1
                bA = bank()
                for h in range(4):
                    mm(psum[:, bA, h * P:(h + 1) * P], Kt_[:, h // 2, seg], Qz_[:, h, seg], True, True,
                       [RK(h // 2), RQ(h)], [PS(bA)])
                bd = bank()
                mcur = pm_sb[:, 2 if t == 16 else 0, :]
                mprev = pm_sb[:, 1, :]
                for g in range(4):
                    mm(psum[:, bd, g * P:(g + 1) * P], ut_[:, 1 + i, g * P:(g + 1) * P], mcur[:, g * P:(g + 1) * P], True,
                       False, [RU(1 + i), "pm0", "pm2"], [PS(bd)])
                    mm(psum[:, bd, g * P:(g + 1) * P], ut_[:, i, g * P:(g + 1) * P], mprev[:, g * P:(g + 1) * P], False,
                       True, [RU(i), "pm1"], [PS(bd)])
                yield
                AT = ATb[i % 2]
                ar = "AT%d" % (i % 2)
                tt(AT, psum[:, bA, :].rearrange("p (h t) -> p h t", t=P), trim, ALU.mult,
                   [PS(bA)] + ["trim%d" % h for h in range(4)], [ar])
                rel(bA)
                act(dT, psum[:, bd, :].rearrange("p (g t) -> p g t", t=P), AF.Copy, [PS(bd)], ["dT"])
                rel(bd)
                yield
                bO = bank()
                for h in range(4):
                    o_ap = psum[:, bO, h * P:(h + 1) * P]
                    mm(o_ap, Vt_[:, i, h * P:(h + 1) * P], AT[:, h, :], True, first, [RV(i), ar], [PS(bO)])
                    if not first:
                        mm(o_ap, Sb[cur][:, h // 2, :], Qz_[:, h, seg], False, True, ["Sb%d" % cur, RQ(h)], [PS(bO)])
                by = bank()
                for g in range(4):
                    mm(psum[:, by, g * P:(g + 1) * P], w_pool_sb[:, g, :], dT[:, g, :], True, True, ["dT", "w_pool"],
                       [PS(by)])
            bs = bank()
            for h in range(4):
                c, po = h // 2, (h % 2) * 64
                mm(psum[po:po + 64, bs, c * P:(c + 1) * P], Ktt_[:, i, h * 64:(h + 1) * 64], Vt_[:, i, h * P:(h + 1) * P],
                   True, True, [RKT(i), RV(i)], [PS(bs)])
            yield
            for c in range(2):
                dcol = dec_[:, c, i:i + 1]
                if first:
                    ts(Sst[nxt][:, c, :], psum[:, bs, c * P:(c + 1) * P], dcol, None, ALU.mult, ALU.bypass,
                       [PS(bs), RD], ["S%d" % nxt])
                else:
                    ts(Sd[:, c, :], Sst[cur][:, c, :], dcol, None, ALU.mult, ALU.bypass, ["S%d" % cur, RD], ["Sd"])
                    stt(Sst[nxt][:, c, :], psum[:, bs, c * P:(c + 1) * P], dcol, Sd[:, c, :], ALU.mult, ALU.add,
                        [PS(bs), RD, "Sd"], ["S%d" % nxt])
            rel(bs)
            act(Sb[nxt], Sst[nxt], AF.Copy, ["S%d" % nxt], ["Sb%d" % nxt])
            state["cur"] = nxt
            state["first"] = False
            if not full:
                yield
                return
            ops_o = psum[:, bO, :].rearrange("p (h t) -> p h t", t=P)
            act(sq, ops_o, AF.Square, [PS(bO)], ["sq"])
            tt(yT[:, 0:4, seg], psum[:, by, :].rearrange("p (g t) -> p g t", t=P), psb, ALU.mult, [PS(by), "psb"], ["yTp%d" % i])
            rel(by)
            yield
            bb = bank()
            for h in range(4):
                mm(psum[:, bb, h * P:(h + 1) * P], ones_bf, sq[:, h, :], True, True, ["sq", "ones_bf"], [PS(bb)])
            yield
            act(rb, psum[:, bb, :].rearrange("p (h t) -> p h t", t=P), AF.Ln, [PS(bb)], ["rb"], bias=EPS, scale=1.0 / P)
            rel(bb)
            act(rb, rb, AF.Exp, ["rb"], ["rb"], scale=-0.5)
            yield
            dma("sp", x1l[s1], x1b, xall[t * P:(t + 1) * P, :], [], [x1r])
            tt(tb, ops_o, rb, ALU.mult, [PS(bO), "rb"], ["tb"])
            rel(bO)
            stt(yT[:, 4:8, seg], tb, c_ggla, sr_[:, :, seg], ALU.mult, ALU.mult,
                ["tb", "cols"] + [RS(h) for h in range(4)], ["yTg%d" % i])
            yield
            bm = bank_pair()
            for half in range(2):
                for k in range(KD):
                    mm(psum[:, bm + half, :], yT[:, k, seg], w_out_sb[:, k, half * 512:(half + 1) * 512], k == 0, k == KD - 1,
                       ["yTp%d" % i, "yTg%d" % i] + W_OUT, [PS(bm + half)])
            yield
            mixps = psum[:, bm:bm + 2, :].rearrange("p a n -> p (a n)")
            ss, ssr = statcol()
            act(m_t1, mixps, AF.Square, [PS(bm), PS(bm + 1)], ["m_t1", ssr], accum=ss)
            r, rres = rstd_from_ss(ss, ssr, 1.0 / D)
            tt(m_t1, mixps, gb_post_mix, ALU.mult, [PS(bm), PS(bm + 1), "gb_post_mix"], ["m_t1"])
            rel(bm, bm + 1)
            yield
            stt(x1b, m_t1, r, x1b, ALU.mult, ALU.add, ["m_t1", rres, x1r], [x1r])
            if t >= 16:
                dma("sp", x1l[s1], out[(t - 16) * P:(t - 15) * P, :], x1b, [x1r], ["out%d" % (t - 16)])
            else:
                ss2, ss2r = statcol()
                act(hx, x1b, AF.Square, [x1r], ["hx", ss2r], accum=ss2)
                r2, r2res = rstd_from_ss(ss2, ss2r, 1.0 / D)
                stt(hx, x1b, r2, gb_pre_ffn, ALU.mult, ALU.mult, [x1r, r2res, "gb_pre_ffn"], ["hx"])
                for k in range(KD):
                    tr(psT[:, k * P:(k + 1) * P], hx[:, k * P:(k + 1) * P], ["hx"], ["psT"])
                act(hh, psT.rearrange("p (k t) -> p k t", t=P)[:, :, P - 2:P], AF.Copy, ["psT"], ["hh"])
            if i == nt - 1 and tiles[-1] != NT_ALL - 1:
                S.add("dve", lambda e: e.tensor_copy(out=ut[1 - par][:, 0, :], in_=ut_[:, nt, :]), [RU(nt)],
                      ["ut0_%d" % (1 - par)])
            yield

        def b_gen():
            lag = 4 if full else 2
            gens = [chunk(i) for i in range(nt)]
            active = []
            nxt_i = 0
            while nxt_i < nt or active:
                if nxt_i < nt and len(active) < 3 and (not active or active[-1][1] >= lag):
                    active.append([gens[nxt_i], 0])
                    nxt_i += 1
                for a_ in list(active):
                    try:
                        next(a_[0])
                        a_[1] += 1
                    except StopIteration:
                        active.remove(a_)
                yield

        n_a2a = 1 + 6 + (4 if full else 2) + nt
        n_a2b = nt + ((nt + 4) if full else 0)
        n_b = (11 + 5 * (nt - 1)) if full else (3 * nt)
        return (prep_gen, 2 * nt + 3), (a2a_gen, n_a2a), (b_gen, n_b), (a2b_gen, n_a2b)

    step_no = [0]

    def run_merged(items, tags, scale=None):
        sid = step_no[0]
        step_no[0] += 1
        gens = [g() for g, _ in items]
        alive = [True] * len(gens)

        def advance(k):
            S.tag = tags[k]
            S.group = []
            try:
                next(gens[k])
                ok = True
            except StopIteration:
                alive[k] = False
                ok = False
            grp, S.group, S.tag = S.group, None, None
            return ok, grp

        if plan is not None:
            for k in plan[sid]:
                if alive[k]:
                    advance(k)
            for k in range(len(gens)):
                while alive[k]:
                    advance(k)
            return
        lens = [max(1, n) for _, n in items]
        if scale:
            lens = [l * scale.get(t, 1.0) for l, t in zip(lens, tags)]
        prog = [0] * len(gens)
        rec = [[] for _ in gens]
        while any(alive):
            k = min((j for j in range(len(gens)) if alive[j]), key=lambda j: (prog[j] + 0.5) / lens[j])
            ok, grp = advance(k)
            if ok:
                prog[k] += 1
                rec[k].append(grp)
        if record is not None:
            record.append((rec, list(tags)))

    def ffn_prep0(fence):
        base = prep_cnt[0]
        pg = [prep_tile_gen(out[i * P:(i + 1) * P, :], f_xa, f_xs, gb_pre_ffn, "gb_pre_ffn", h2T, "h2T", i * P, "f",
                            extra=fence, rstd_fn=rstd_from_ss, do_load=False) for i in range(4)]
        prog = [0] * 4
        live = []
        nx = 0
        while nx < 4 or live:
            if nx < 4 and len(live) < 2:
                live.append(nx)
                nx += 1
            for i in list(live):
                try:
                    next(pg[i])
                    prog[i] += 1
                    if prog[i] == 2 and i + 2 < 4:
                        s_ = (base + i) % 2
                        dma("sp", xlf[s_], f_xa[s_], out[(i + 2) * P:(i + 3) * P, :], [], ["fxa%d" % s_], extra=fence)
                except StopIteration:
                    live.remove(i)
            yield

    for par in range(2):
        S.add("dve", lambda e, par=par: e.memset(ut[par][:, 0, :], 0.0), [], ["ut0_%d" % par])
        for h_ in range(4):
            rows = slice(64, 128) if h_ % 2 == 0 else slice(0, 64)
            S.add("dve", lambda e, par=par, h_=h_, rows=rows: e.memset(Qz[par][rows, h_, :], 0.0), [],
                  ["Qzz%d_%d" % (h_, par)])
    load_class(0)
    macros = []
    pref = list(range(0, HALO_T))
    if "no_prefix" not in opts:
        for m0 in range(0, len(pref), 4):
            macros.append((pref[m0:m0 + 4], False))
    for tl in ([15, 16, 17, 18], [19, 20, 21, 22], [23, 24, 25, 26], [27, 28, 29, 30], [31]):
        macros.append((tl, "no_full" not in opts))
    built = [make_macro(tl, fl, idx % 2) for idx, (tl, fl) in enumerate(macros)]
    nM = len(built)
    for t0_ in range(3):
        m_load(t0_)
    dma("sp", newlane("c"), gb_post_mix, gbs[1], [], ["gb_post_mix"])
    dma("sp", newlane("c"), gb_pre_ffn, gbs[2], [], ["gb_pre_ffn"])
    dma("sp", newlane("c"), gb_post_ffn, gbs[3], [], ["gb_post_ffn"])
    dma("sp", newlane("c"), psb, psb_d, [], ["psb"])
    for s in range(-2, nM):
        items, tags = [], []
        if 0 <= s + 2 < nM:
            items.append(built[s + 2][0])
            tags.append("prep")
        if 0 <= s + 1 < nM:
            items.append(built[s + 1][1])
            tags.append("a2")
            items.append(built[s + 1][3])
            tags.append("a2")
        if 0 <= s < nM:
            items.append(built[s][2])
            tags.append("b")
        if s == nM - 1 and not stop_after_mixer:
            assert (nM - 1) % 2 == 0
            for i_ in range(2):
                s_ = (prep_cnt[0] + i_) % 2
                dma("sp", xlf[s_], f_xa[s_], out[i_ * P:(i_ + 1) * P, :], [], ["fxa%d" % s_], extra=fenceP)
            items.append((lambda: ffn_prep0(fenceP), 12))
            tags.append("fprep")
        run_merged(items, tags, scale={"a2": 3.0} if s == nM - 2 else None)
        if s == nM - 2:
            fenceP = S.fence()
            fenceA = S.fence_tags(["prep", "a2"])
            load_class(1, extra=fenceA)
    fenceM = S.fence()

    def fblock_of(f):
        for bi, (f0, nf, g, v, o, _) in enumerate(fblk):
            if f0 <= f < f0 + nf:
                return bi, f - f0
        raise AssertionError

    X = fenceM
    def f_prep(m, i):
        prep_tile(out[(4 * m + i) * P:(4 * m + i + 1) * P, :], f_xa, f_xs, gb_pre_ffn, "gb_pre_ffn", h2T, "h2T", i * P,
                  "f", extra=X)

    load_class(2, extra=fenceM)
    for m in range(0 if stop_after_mixer else 4):
        halc, haln = hal[m % 2], hal[(m + 1) % 2]
        hcr, hnr = "hal%d" % (m % 2), "hal%d" % ((m + 1) % 2)
        bh = None
        if m == 0:
            bh = bank()
        for f in range(NF):
            bi, fi = fblock_of(f)
            _, _, gw, vw, ow, _ = fblk[bi]
            bgt, bvl = bank(), bank()
            for k in range(KD):
                mm(psum[:, bgt, :], gw[:, k, fi * P:(fi + 1) * P], h2T[:, k, :], k == 0, k == KD - 1,
                   ["h2T"] + FWG(bi), [PS(bgt)], extra=X, wfuse=True)
            if m == 0:
                for k in range(KD):
                    mm(psum[:, bh, 2 * f:2 * f + 2], gw[:, k, fi * P:(fi + 1) * P], hh[:, k, :], k == 0, k == KD - 1,
                       ["hh"] + FWG(bi), [PS(bh)], extra=X)
                act(halc[:, f, :], psum[:, bh, 2 * f:2 * f + 2], AF.Copy, [PS(bh)], [hcr], extra=X)
            for k in range(KD):
                mm(psum[:, bvl, :], vw[:, k, fi * P:(fi + 1) * P], h2T[:, k, :], k == 0, k == KD - 1,
                   ["h2T"] + FWV(bi), [PS(bvl)], extra=X, wfuse=True)
            acc, accr = accb[f % 2], "acc%d" % (f % 2)
            gl, glr = glb[f % 2], "gl%d" % (f % 2)
            gps = psum[:, bgt, :]
            act(acc, gps, AF.Identity, [PS(bgt), "cols"], [accr], bias=c_cb[:, f:f + 1], scale=c_cw[:, f, 2:3], extra=X)
            act(haln[:, f, :], gps[:, 510:512], AF.Copy, [PS(bgt)], [hnr], extra=X)
            stt(acc[:, 1:512], gps[:, 0:511], c_cw[:, f, 1:2], acc[:, 1:512], ALU.mult, ALU.add, [PS(bgt), "cols", accr],
                [accr], extra=X)
            stt(acc[:, 2:512], gps[:, 0:510], c_cw[:, f, 0:1], acc[:, 2:512], ALU.mult, ALU.add, [PS(bgt), "cols", accr],
                [accr], extra=X)
            rel(bgt)
            stt(acc[:, 0:1], halc[:, f, 1:2], c_cw[:, f, 1:2], acc[:, 0:1], ALU.mult, ALU.add, [hcr, "cols", accr],
                [accr], extra=X)
            stt(acc[:, 0:2], halc[:, f, 0:2], c_cw[:, f, 0:1], acc[:, 0:2], ALU.mult, ALU.add, [hcr, "cols", accr],
                [accr], extra=X)
            act(gl, acc, AF.Gelu, [accr], [glr], extra=X)
            tt(actb[:, f, :], gl, psum[:, bvl, :], ALU.mult, [glr, PS(bvl)], ["actb%d" % f], extra=X)
            rel(bvl)
        if bh is not None:
            rel(bh)
        def f_load(i_):
            s_ = (prep_cnt[0] + (i_ - f_load.base)) % 2
            dma("sp", xlf[s_], f_xa[s_], out[(4 * m + 4 + i_) * P:(4 * m + 5 + i_) * P, :], [], ["fxa%d" % s_], extra=X)

        if m < 3:
            f_load.base = 0
            f_load(0)
            f_load(1)
        for i in range(4):
            ti = 4 * m + i
            pgen = None
            if m < 3:
                pgen = prep_tile_gen(out[(4 * m + 4 + i) * P:(4 * m + 5 + i) * P, :], f_xa, f_xs, gb_pre_ffn, "gb_pre_ffn",
                                     h2T, "h2T", i * P, "f", extra=X, do_load=False)
                next(pgen)
                next(pgen)
                if i + 2 < 4:
                    f_load.base = i + 1
                    f_load(i + 2)
            bo_ = bank_pair()
            for f in range(NF):
                bi, fi = fblock_of(f)
                ow = fblk[bi][4]
                for half in range(2):
                    mm(psum[:, bo_ + half, :], actb[:, f, i * P:(i + 1) * P], ow[:, fi, half * 512:(half + 1) * 512],
                       f == 0, f == NF - 1, ["actb%d" % f, "fwo%d" % bi], [PS(bo_ + half)], extra=X)
            if pgen is not None:
                for _ in pgen:
                    pass
            ffps = psum[:, bo_:bo_ + 2, :].rearrange("p a n -> p (a n)")
            s1 = ti % 2
            x1b, x1r = f_x1[s1], "fx1%d" % s1
            dma("sp", x1l[s1], x1b, out[ti * P:(ti + 1) * P, :], ["out%d" % ti], [x1r], extra=X)
            ss, ssr = statcol()
            act(f_t1, ffps, AF.Square, [PS(bo_), PS(bo_ + 1)], ["f_t1", ssr], accum=ss, extra=X)
            r, rres = rstd_pool(ss, ssr, 1.0 / D, extra=X)
            tt(f_t1, ffps, gb_post_ffn, ALU.mult, [PS(bo_), PS(bo_ + 1), "gb_post_ffn"], ["f_t1"], extra=X)
            rel(bo_, bo_ + 1)
            stt(x1b, f_t1, r, x1b, ALU.mult, ALU.add, ["f_t1", rres, x1r], [x1r], extra=X)
            dma("sp", x1l[s1], out[ti * P:(ti + 1) * P, :], x1b, [x1r], ["out%d" % ti], extra=X)
    final_deps = S.fence()
    S.add("sp", lambda e: e.nop(), extra=final_deps)

    S.finalize()
    sems = {}
    for e in ("pe", "act", "dve", "pool"):
        sems[e] = es.enter_context(nc.semaphore("s_" + e))
    for lane in S.lane_cnt:
        sems[("L", lane)] = es.enter_context(nc.semaphore("l_" + lane))
    with nc.Block() as block:
        @block.tensor
        def _(e):
            S.emit_engine("pe", e, sems)

        @block.scalar
        def _(e):
            S.emit_engine("act", e, sems)

        @block.vector
        def _(e):
            S.emit_engine("dve", e, sems)

        @block.gpsimd
        def _(e):
            S.emit_engine("pool", e, sems)

        @block.sync
        def _(e):
            S.emit_engine("sp", e, sems)
    es.close()
    build_program.info = dict(M_BASE=M_BASE, A_END=A_END, M_USED=M_USED, gv_class=gv_class, wo_class=wo_class)
    return nc


def make_plan(record, slack=1.0, rng=None, noise=0.0):
    eng_free = {}
    finish = {}
    plan = []
    nsteps = len(record)
    owner = {}
    for sid_, (rec_, tags_) in enumerate(record):
        for k_, stream in enumerate(rec_):
            for grp in stream:
                for op in grp:
                    owner[op[0]] = (sid_, k_)

    op_eng = {}

    def dep_time(deps, sid, k, eng):
        t = 0.0
        for d in deps:
            o = owner.get(d)
            if o is not None and o[0] == sid and o[1] != k:
                continue
            if d in finish:
                t = max(t, finish[d] + (0.0 if op_eng.get(d) == eng else 0.8))
        return t

    def chain_cost(grp):
        per = {}
        for (idx, eng, cost, deps, is_dma) in grp:
            per[eng] = per.get(eng, 0.0) + (0.06 if is_dma else cost)
        return (max(per.values()) if per else 0.0) + 0.4

    for sid, (rec, tags) in enumerate(record):
        ptr = [0] * len(rec)
        order = []
        remaining = [sum(chain_cost(g) for g in r) for r in rec]
        boost = {k: (1e6 if (sid == nsteps - 2 and tags[k] == "a2") else 0.0) for k in range(len(rec))}
        while True:
            cands = [k for k in range(len(rec)) if ptr[k] < len(rec[k])]
            if not cands:
                break
            sts = {}
            for k in cands:
                grp = rec[k][ptr[k]]
                if not grp:
                    sts[k] = -1e9
                else:
                    idx, eng, cost, deps, is_dma = grp[0]
                    sts[k] = max(eng_free.get(eng, 0.0), dep_time(deps, sid, k, eng))
            tmin = min(sts.values())
            near = [k for k in cands if sts[k] <= tmin + slack]
            if rng is not None:
                k = max(near, key=lambda j: (remaining[j] * (1.0 + noise * (2.0 * rng.random() - 1.0)) + boost[j], -j))
            else:
                k = max(near, key=lambda j: (remaining[j] + boost[j], -j))
            for (idx, eng, cost, deps, is_dma) in rec[k][ptr[k]]:
                st = max(eng_free.get(eng, 0.0), dep_time(deps, sid, k, eng))
                op_eng[idx] = "dma" if is_dma else eng
                if is_dma:
                    eng_free[eng] = st + 0.06
                    finish[idx] = st + cost
                else:
                    eng_free[eng] = st + cost
                    finish[idx] = st + cost
            remaining[k] -= chain_cost(rec[k][ptr[k]])
            ptr[k] += 1
            order.append(k)
        plan.append(order)
        make_plan.step_ends = getattr(make_plan, "step_ends", [])
        if sid == 0:
            make_plan.step_ends = []
        make_plan.step_ends.append(dict(eng_free))
    make_plan.model_time = max(eng_free.values()) if eng_free else 0.0
    return plan


def search_plan(record, trials=60):
    import random
    best = make_plan(record)
    best_t = make_plan.model_time
    rng = random.Random(1234)
    for i in range(trials):
        sl = (0.5, 1.0, 1.5, 2.0)[i % 4]
        p = make_plan(record, slack=sl, rng=rng, noise=0.35)
        if make_plan.model_time < best_t:
            best, best_t = p, make_plan.model_time
    search_plan.model_time = best_t
    return best


def _pool_mats(first_half):
    wins = (2, 4, 8, 16)
    mats = np.zeros((3, 4, P, P), np.float32)
    j = np.arange(P)[:, None]
    i = np.arange(P)[None, :]
    for g, w in enumerate(wins):
        band = ((j <= i) & (j > i - w)).astype(np.float32)
        mats[0, g] = band / w - np.eye(P, dtype=np.float32)
        mats[1, g] = ((j + 0 > P + i - w)).astype(np.float32) / w
        if first_half:
            cnt = np.minimum(np.arange(1, P + 1), w).astype(np.float32)[None, :]
            mats[2, g] = band / cnt - np.eye(P, dtype=np.float32)
        else:
            mats[2, g] = mats[0, g]
    return mats


_NC_CACHE = {}


def kernel(x, g_pre_mix, w_in, w_pool, pool_scale, w_gate_up, b_gate, g_gla_norm, w_out,
           g_post_mix, g_pre_ffn, w_ffn_in, conv_w, conv_b, w_ffn_out, g_post_ffn):
    f = lambda a: np.ascontiguousarray(np.asarray(a, dtype=np.float32))
    x = f(x)
    B, SEQ, _ = x.shape
    gbs = np.stack([np.broadcast_to(f(g)[None, :], (P, D)) for g in (g_pre_mix, g_post_mix, g_pre_ffn, g_post_ffn)])
    cols = np.zeros((P, NCOL), np.float32)
    cols[:, 0:2] = f(b_gate).reshape(2, P).T
    cols[:, 2] = f(g_gla_norm)
    cols[:, 3:7] = f(pool_scale).reshape(4, P).T
    cw = f(conv_w)
    cols[:, 7:73] = cw.reshape(3, NF, P).transpose(2, 1, 0).reshape(P, NF * 3)
    cols[:, 73:95] = f(conv_b).reshape(NF, P).T
    cmask = np.zeros((P, 3, P), np.float32)
    cmask[:, 0, :] = np.eye(P)
    cmask[:, 1, :] = (np.arange(P)[:, None] <= np.arange(P)[None, :])
    cmask[:, 2, :] = 1.0
    psb = np.ascontiguousarray(np.broadcast_to(f(pool_scale).reshape(4, P).T[:, :, None], (P, 4, P)))
    shared = {
        "w_in": f(w_in), "w_out": f(w_out), "w_ffn_in": f(w_ffn_in), "w_ffn_out": f(w_ffn_out),
        "w_pool": f(w_pool), "w_gu": f(w_gate_up), "gbs": np.ascontiguousarray(gbs), "cols": cols,
        "cmask": cmask, "psb": psb,
    }
    pm = {True: _pool_mats(True), False: _pool_mats(False)}
    in_maps = []
    half = SEQ // 2
    for c in range(8):
        b, hf = c // 2, c % 2
        if hf == 0:
            xa = np.concatenate([np.zeros((half, D), np.float32), x[b, :half]], axis=0)
        else:
            xa = x[b]
        m = dict(shared)
        m["xall"] = np.ascontiguousarray(xa)
        m["pmats"] = pm[hf == 0]
        in_maps.append(m)
    if "nc" not in _NC_CACHE:
        rec = []
        build_program(record=rec)
        _NC_CACHE["nc"] = build_program(plan=search_plan(rec))
    res = run_bass_kernel_spmd(_NC_CACHE["nc"], in_maps, core_ids=list(range(8)))
    outp = np.empty((B, SEQ, D), np.float32)
    for c in range(8):
        b, hf = c // 2, c % 2
        outp[b, hf * half:(hf + 1) * half] = res.results[c]["out"]
    return outp
```

```python
import numpy as np
from contextlib import ExitStack
import concourse.bass as bass
import concourse.mybir as mybir
from concourse.bass_utils import run_bass_kernel_spmd

F32 = mybir.dt.float32
BF16 = mybir.dt.bfloat16
U8 = mybir.dt.uint8
AF = mybir.ActivationFunctionType
ALU = mybir.AluOpType

P = 128
D = 1024
KD = 8
DIN = 2064
DFF = 2816
NF = 22
NT_ALL = 32
HALO_T = 15
EPS = 1e-6
C_U, C_Q, C_K, C_V, C_G, C_R = 0, 512, 768, 1024, 1536, 1552
FBLOCKS = [(0, 2), (2, 4), (6, 4), (10, 4), (14, 4), (18, 4)]
NCOL = 2 + 1 + 4 + 66 + 22


class _Op:
    __slots__ = ("eng", "emit", "deps", "lane", "idx", "eidx", "signal", "sigcount", "lanecount", "waits", "fuse")


class Sched:
    ENG = ("pe", "act", "dve", "pool", "sp")

    def __init__(self):
        self.ops = []
        self.byeng = {e: [] for e in self.ENG}
        self.lastw = {}
        self.readers = {}
        self.lane_last = {}
        self.lane_cnt = {}
        self.dom_last = {}
        self.tag = None
        self.tag_last = {}
        self.group = None

    def fence_tags(self, tags):
        out = set()
        for t in tags:
            out.update(self.tag_last.get(t, {}).values())
        return out

    def _dom(self, op):
        return ("L", op.lane) if op.lane is not None else op.eng

    def add(self, eng, emit, reads=(), writes=(), lane=None, extra=(), cost=0.3, fuse=None):
        idx = len(self.ops)
        deps = set(extra)
        for r in reads:
            if r in self.lastw:
                deps.add(self.lastw[r])
        for w in writes:
            if w in self.lastw:
                deps.add(self.lastw[w])
            deps.update(self.readers.get(w, {}).values())
        if lane is not None and lane in self.lane_last:
            deps.add(self.lane_last[lane])
        deps.discard(idx)
        op = _Op()
        op.eng, op.emit, op.deps, op.lane, op.idx = eng, emit, deps, lane, idx
        op.signal, op.sigcount, op.lanecount, op.waits = False, 0, 0, None
        op.fuse = (eng in ("dve", "act") and lane is None) if fuse is None else fuse
        dom = ("L", lane) if lane is not None else eng
        for r in reads:
            self.readers.setdefault(r, {})[dom] = idx
        for w in writes:
            self.lastw[w] = idx
            self.readers[w] = {}
        if lane is not None:
            self.lane_last[lane] = idx
            self.lane_cnt[lane] = self.lane_cnt.get(lane, 0) + 16
            op.lanecount = self.lane_cnt[lane]
        op.eidx = len(self.byeng[eng])
        self.ops.append(op)
        self.byeng[eng].append(op)
        self.dom_last[dom] = idx
        self.tag_last.setdefault(self.tag, {})[dom] = idx
        if self.group is not None:
            self.group.append((idx, eng, cost, tuple(deps), lane is not None))
        return idx

    def fence(self):
        return set(self.dom_last.values())

    def _needed(self, x, d):
        if d.lane is not None:
            return True
        if d.eng == x.eng and x.lane is None:
            return x.eng != "pe"
        return True

    def finalize(self):
        ops = self.ops

        def dom(o):
            return ("L", o.lane) if o.lane is not None else o.eng

        know = {e: {} for e in self.ENG}
        after = [None] * len(ops)
        needed = [None] * len(ops)
        for x in ops:
            E = x.eng
            K = dict(know[E])
            nd = []
            for di in sorted(x.deps, reverse=True):
                d = ops[di]
                if d.lane is None and x.lane is None and d.eng == "pe" and E == "pe":
                    continue
                dm = dom(d)
                if K.get(dm, -1) >= di:
                    continue
                nd.append(di)
                for k2, v2 in after[di].items():
                    if K.get(k2, -1) < v2:
                        K[k2] = v2
            needed[x.idx] = nd
            know[E] = K
            a_ = dict(K)
            dx = dom(x)
            if a_.get(dx, -1) < x.idx:
                a_[dx] = x.idx
            after[x.idx] = a_
        for x in ops:
            for di in needed[x.idx]:
                if ops[di].lane is None:
                    ops[di].signal = True
        for e in self.ENG:
            c = 0
            for op in self.byeng[e]:
                if op.lane is None and op.signal:
                    c += 1
                op.sigcount = c
        for x in ops:
            w = {}
            for di in needed[x.idx]:
                d = ops[di]
                if d.lane is not None:
                    key, val = ("L", d.lane), d.lanecount
                else:
                    key, val = d.eng, d.sigcount
                if w.get(key, 0) < val:
                    w[key] = val
            x.waits = w

    def emit_engine(self, e, engine, sems):
        for x in self.byeng[e]:
            w = list(x.waits.items())
            fuse_ok = bool(x.fuse)
            if x.fuse == "w" and any(isinstance(k, tuple) for k, _ in w):
                fuse_ok = False
            fused = w.pop() if (w and fuse_ok) else None
            for k, v in w:
                engine.wait_ge(sems[k], v)
            ins = x.emit(engine)
            if fused is not None:
                ins._wait_ge(sems[fused[0]], fused[1])
            if x.lane is not None:
                ins.then_inc(sems[("L", x.lane)], 16)
            elif x.signal:
                ins.then_inc(sems[e], 1)


def build_program(stop_after_mixer=False, plan=None, record=None, opts=()):
    nc = bass.Bass("TRN2", target_bir_lowering=False)

    def din(name, shape, dt=F32):
        return nc.dram_tensor(name, list(shape), dt, kind="ExternalInput").ap()

    xall = din("xall", [NT_ALL * P, D])
    w_in = din("w_in", [D, DIN])
    w_out = din("w_out", [D, D])
    w_fi = din("w_ffn_in", [D, 2 * DFF])
    w_fo = din("w_ffn_out", [DFF, D])
    w_pool = din("w_pool", [4, P, P])
    w_gu = din("w_gu", [16, 256])
    gbs = din("gbs", [4, P, D])
    cols_d = din("cols", [P, NCOL])
    cmask = din("cmask", [P, 3, P])
    pmats = din("pmats", [3, 4, P, P])
    psb_d = din("psb", [P, 4, P])
    out = nc.dram_tensor("out", [16 * P, D], F32, kind="ExternalOutput").ap()

    S = Sched()
    TOTAL = 212480
    es = ExitStack()
    big = es.enter_context(nc.sbuf_tensor("big", [P, TOTAL], U8))
    psum = es.enter_context(nc.psum_tensor("ps", [P, 8, 512], F32))

    def carve(off, shape, dt):
        n = int(np.prod(shape[1:]))
        sz = n * (4 if dt == F32 else 2)
        ap = big[0:shape[0], off:off + sz].bitcast(dt)
        if len(shape) == 3:
            ap = ap.rearrange("p (a b) -> p a b", b=shape[2])
        elif len(shape) == 4:
            ap = ap.rearrange("p (a b c) -> p a b c", b=shape[2], c=shape[3])
        return ap

    class Arena:
        def __init__(self, base, limit):
            self.off, self.limit = base, limit

        def get(self, shape, dt):
            n = int(np.prod(shape[1:])) * (4 if dt == F32 else 2)
            n = (n + 63) // 64 * 64
            o = self.off
            self.off += n
            assert self.off <= self.limit, (self.off, self.limit)
            return carve(o, shape, dt)

    PERS = 10240
    pa = Arena(0, PERS)
    ident = pa.get([P, P], BF16)
    ones_bf = pa.get([P, P], BF16)
    gb_pre_ffn = pa.get([P, D], F32)
    gb_post_ffn = pa.get([P, D], F32)
    cols = pa.get([P, NCOL], F32)
    negb = pa.get([P, 2], F32)
    hal = [pa.get([P, NF, 2], F32) for _ in range(2)]
    hh = pa.get([P, KD, 2], BF16)
    stat = pa.get([P, 64], F32)
    mhalf = pa.get([P, 1], F32)
    c_bg = cols[:, 0:2]
    c_ggla = cols[:, 2:3]
    c_psc = cols[:, 3:7]
    c_cw = cols[:, 7:73].rearrange("p (f t) -> p f t", t=3)
    c_cb = cols[:, 73:95]

    FW0 = PERS
    fblk = []
    off = FW0
    gv_off = []
    for (f0, nf) in FBLOCKS:
        gv_off.append(off)
        off += 2 * KD * nf * P * 2
    wo_off = []
    for (f0, nf) in FBLOCKS:
        wo_off.append(off)
        off += nf * D * 2
    for bi, (f0, nf) in enumerate(FBLOCKS):
        g = carve(gv_off[bi], [P, KD, nf * P], BF16)
        v = carve(gv_off[bi] + KD * nf * P * 2, [P, KD, nf * P], BF16)
        o = carve(wo_off[bi], [P, nf, D], BF16)
        fblk.append((f0, nf, g, v, o, gv_off[bi]))
    FW_END = off
    assert FW_END == PERS + 135168

    M_SIZE = 193920
    M_BASE = TOTAL - M_SIZE
    ma = Arena(M_BASE, TOTAL)
    w_in_sb = ma.get([P, KD, DIN], BF16)
    hT2 = [ma.get([P, KD, 512], BF16) for _ in range(2)]
    m_xa = [ma.get([P, D], F32) for _ in range(3)]
    setmp = [ma.get([P, 512], F32) for _ in range(2)]
    m_xs = [ma.get([P, D], BF16) for _ in range(2)]
    glT = ma.get([16, 512], BF16)
    lg = ma.get([P, 2, 512], F32)
    braw = ma.get([P, 2, 512], F32)
    Ek = ma.get([P, 2, 512], F32)
    w_gu_sb = ma.get([16, 256], BF16)
    gb_pre_mix = ma.get([P, D], F32)
    ones_f = ma.get([P, P], F32)
    A_END = ma.off
    w_out_sb = ma.get([P, KD, D], BF16)
    w_pool_sb = ma.get([P, 4, P], BF16)
    gb_post_mix = ma.get([P, D], F32)
    trim = ma.get([P, 4, P], BF16)
    pm_sb = ma.get([P, 3, 4 * P], BF16)
    psb = ma.get([P, 4, P], F32)
    m_x1 = [ma.get([P, D], F32) for _ in range(2)]
    hx = ma.get([P, D], BF16)
    _shapes = (("ut", [P, 5, 512], BF16), ("Vt", [P, 4, 512], BF16), ("Qz", [P, 4, 512], BF16),
               ("Kt", [P, 2, 512], BF16), ("Ktt", [P, 4, 256], BF16), ("sr", [P, 4, 512], BF16), ("dec", [P, 2, 4], F32))
    _p0 = {n: ma.get(s, d) for n, s, d in _shapes}
    ATb = [ma.get([P, 4, P], BF16) for _ in range(2)]
    m_x1l = None
    sq = ma.get([P, 4, P], BF16)
    rb = ma.get([P, 4, P], F32)
    tb = ma.get([P, 4, P], F32)
    dT = ma.get([P, 4, P], BF16)
    yT = ma.get([P, KD, 512], BF16)
    m_t1 = ma.get([P, D], F32)
    Sst = [ma.get([P, 2, P], F32) for _ in range(2)]
    Sd = ma.get([P, 2, P], F32)
    Sb = [ma.get([P, 2, P], BF16) for _ in range(2)]
    P1_BASE = ma.off
    _p1 = {n: ma.get(s, d) for n, s, d in _shapes}
    P1_END = ma.off
    ut, Vt, Qz, Kt, Ktt, sr, dec = ([_p0[n], _p1[n]] for n in ("ut", "Vt", "Qz", "Kt", "Ktt", "sr", "dec"))
    def _cls(bend):
        return 0 if bend <= M_BASE else (1 if bend <= A_END else 2)
    gv_class = [_cls(gv_off[bi] + 2 * KD * nf * P * 2) for bi, (f0, nf) in enumerate(FBLOCKS)]
    wo_class = [_cls(wo_off[bi] + nf * D * 2) for bi, (f0, nf) in enumerate(FBLOCKS)]
    M_USED = ma.off

    fa = Arena(FW_END, P1_BASE)
    fp1 = Arena(P1_BASE, P1_END)
    actb = fa.get([P, NF, 512], BF16)
    f_xa = [fp1.get([P, D], F32) for _ in range(2)]
    f_x1 = [fa.get([P, D], F32) for _ in range(2)]
    f_xs = [fp1.get([P, D], BF16) for _ in range(2)]
    h2T = fp1.get([P, KD, 512], BF16)
    accb = [fa.get([P, 512], F32) for _ in range(2)]
    glb = [fa.get([P, 512], F32) for _ in range(2)]
    f_t1 = fa.get([P, D], F32)

    psT = psum[:, 7, :].bitcast(BF16)
    bank_rr = [0]
    held = set()

    def bank():
        for _ in range(8):
            b = bank_rr[0] % 7
            bank_rr[0] += 1
            if b not in held:
                held.add(b)
                return b
        raise AssertionError("no free PSUM bank: %r" % (held,))

    def bank_pair():
        for _ in range(16):
            b = bank_rr[0] % 7
            if b < 6 and b not in held and (b + 1) not in held:
                bank_rr[0] += 2
                held.add(b)
                held.add(b + 1)
                return b
            bank_rr[0] += 1
        raise AssertionError("no free PSUM bank pair: %r" % (held,))

    def rel(*bs):
        for b in bs:
            held.discard(b)

    def PS(b):
        return "ps%d" % b

    stat_rr = [0]

    def statcol(n=1):
        c = stat_rr[0] % 56
        if c + n > 56:
            c = 0
            stat_rr[0] = 0
        stat_rr[0] += n
        return stat[:, c:c + n], "stat%d" % c

    def _n(ap):
        return float(ap.free_size())

    def dma(eng, lane, out_ap, in_ap, reads, writes, extra=()):
        return S.add(eng, lambda e: e.dma_start(out=out_ap, in_=in_ap), reads, writes, lane=lane, extra=extra, cost=3.0)

    def act(out_ap, in_ap, func, reads, writes, bias=None, scale=None, accum=None, extra=()):
        kw = {}
        if bias is not None:
            kw["bias"] = bias
        if scale is not None:
            kw["scale"] = scale
        if accum is not None:
            kw["accum_out"] = accum
        return S.add("act", lambda e: e.activation(out=out_ap, in_=in_ap, func=func, **kw), reads, writes, extra=extra,
                     cost=0.25 + _n(in_ap) / 1200.0 + (0.1 if accum is not None else 0.0),
                     fuse=(accum is None))

    def tt(out_ap, a, b, op, reads, writes, extra=()):
        return S.add("dve", lambda e: e.tensor_tensor(out=out_ap, in0=a, in1=b, op=op), reads, writes, extra=extra,
                     cost=0.12 + _n(a) / 830.0)

    def ts(out_ap, a, s1, s2, op0, op1, reads, writes, extra=()):
        return S.add("dve", lambda e: e.tensor_scalar(out=out_ap, in0=a, scalar1=s1, scalar2=s2, op0=op0, op1=op1),
                     reads, writes, extra=extra, cost=0.12 + _n(a) / 900.0)

    def stt(out_ap, a, sc, b, op0, op1, reads, writes, extra=()):
        return S.add("dve", lambda e: e.scalar_tensor_tensor(out=out_ap, in0=a, scalar=sc, in1=b, op0=op0, op1=op1),
                     reads, writes, extra=extra, cost=0.12 + _n(a) / 830.0)

    def mm(out_ap, lhsT, rhs, start, stop, reads, writes, extra=(), wfuse=False):
        return S.add("pe", lambda e: e.matmul(out_ap, lhsT, rhs, start=start, stop=stop), reads, writes, extra=extra,
                     cost=0.04 + 0.00039 * _n(rhs), fuse=("w" if wfuse else False))

    def tr(out_ap, in_ap, reads, writes, extra=()):
        return S.add("pe", lambda e: e.transpose(out_ap, in_ap, ident), reads + ["ident"], writes, extra=extra, cost=0.11)

    def rstd_pool(ss_ap, ss_res, scale, extra=()):
        tmp, tres = statcol()
        S.add("pool", lambda e: e.tensor_scalar(out=tmp, in0=ss_ap, scalar1=scale, scalar2=EPS, op0=ALU.mult, op1=ALU.add),
              [ss_res], [tres], extra=extra)
        r, rres = statcol()
        S.add("pool", lambda e: e.tensor_tensor(out=r, in0=tmp, in1=mhalf, op=ALU.pow), [tres, "mhalf"], [rres], extra=extra)
        return r, rres

    def rstd_from_ss(ss_ap, ss_res, scale, extra=()):
        tmp, tres = statcol()
        act(tmp, ss_ap, AF.Ln, [ss_res], [tres], bias=EPS, scale=scale, extra=extra)
        r, rres = statcol()
        act(r, tmp, AF.Exp, [tres], [rres], scale=-0.5, extra=extra)
        return r, rres

    lane_id = [0]

    def newlane(prefix):
        lane_id[0] += 1
        return "%s%d" % (prefix, lane_id[0])

    LC = newlane("c")
    dma("sp", newlane("c"), cols, cols_d, [], ["cols"])
    dma("sp", newlane("c"), gb_pre_mix, gbs[0], [], ["gb_pre_mix"])
    wl = [newlane("w") for _ in range(8)]
    wl_i = [0]

    def wdma(out_ap, in_ap, writes, extra=()):
        lane = wl[wl_i[0] % len(wl)]
        wl_i[0] += 1
        return dma("pool", lane, out_ap, in_ap, [], writes, extra=extra)

    wdma(ident, cmask[:, 0, :], ["ident"])
    wdma(w_gu_sb, w_gu, ["w_gu"])
    for h_ in range(4):
        wdma(trim[:, h_, :], cmask[:, 1, :], ["trim%d" % h_])
    wdma(ones_bf, cmask[:, 2, :], ["ones_bf"])
    w_in_v = w_in.rearrange("(k p) n -> p k n", p=P)
    for (c0, c1, nm) in ((C_K, C_R, "w_in_a"), (C_R, DIN, "w_in_c"), (0, C_K, "w_in_b")):
        for k2 in range(0, KD, 2):
            wdma(w_in_sb[:, k2:k2 + 2, c0:c1], w_in_v[:, k2:k2 + 2, c0:c1], ["%s%d" % (nm, k2)])
    W_IN_A = ["w_in_a%d" % k for k in range(0, KD, 2)]
    W_IN_B = ["w_in_b%d" % k for k in range(0, KD, 2)]
    W_IN_C = ["w_in_c%d" % k for k in range(0, KD, 2)]
    S.add("dve", lambda e: e.memset(ones_f, 1.0), [], ["ones_f"])
    S.add("dve", lambda e: e.memset(mhalf, -0.5), [], ["mhalf"])
    ts(negb, c_bg, -1.0, None, ALU.mult, ALU.bypass, ["cols"], ["negb"])
    for g in range(3):
        wdma(pm_sb[:, g, :].rearrange("p (a b) -> p a b", b=P), pmats[g].rearrange("g j i -> j g i"), ["pm%d" % g])
    wdma(w_pool_sb, w_pool.rearrange("g c d -> c g d"), ["w_pool"])
    w_out_v = w_out.rearrange("(k p) n -> p k n", p=P)
    for k2 in range(0, KD, 4):
        wdma(w_out_sb[:, k2:k2 + 4, :], w_out_v[:, k2:k2 + 4, :], ["w_out%d" % k2])
    W_OUT = ["w_out0", "w_out4"]

    w_fi_v = w_fi.rearrange("(k p) n -> p k n", p=P)

    def load_gv(bi, extra=()):
        f0, nf, g, v, o, _ = fblk[bi]
        for k2 in range(0, KD, 4):
            wdma(g[:, k2:k2 + 4, :], w_fi_v[:, k2:k2 + 4, f0 * P:(f0 + nf) * P], ["fwg%d_%d" % (bi, k2)], extra=extra)
            wdma(v[:, k2:k2 + 4, :], w_fi_v[:, k2:k2 + 4, DFF + f0 * P:DFF + (f0 + nf) * P],
                 ["fwv%d_%d" % (bi, k2)], extra=extra)

    def load_wo(bi, extra=()):
        f0, nf, g, v, o, _ = fblk[bi]
        wdma(o, w_fo[f0 * P:(f0 + nf) * P, :].rearrange("(f p) n -> p f n", p=P), ["fwo%d" % bi], extra=extra)

    def load_class(cls, extra=()):
        for bi in range(len(fblk)):
            if gv_class[bi] == cls:
                load_gv(bi, extra=extra)
        for bi in range(len(fblk)):
            if wo_class[bi] == cls:
                load_wo(bi, extra=extra)

    def FWG(bi):
        return ["fwg%d_0" % bi, "fwg%d_4" % bi]

    def FWV(bi):
        return ["fwv%d_0" % bi, "fwv%d_4" % bi]

    xl = [newlane("x") for _ in range(4)]
    xlf = [newlane("xf") for _ in range(2)]
    x1l = [newlane("y") for _ in range(2)]
    prep_cnt = [0]

    def prep_tile_gen(src_ap, xa_bufs, xs_bufs, gb, gbres, dstT, dst_res, col0, tag, extra=(), rstd_fn=None,
                      do_load=True):
        s = prep_cnt[0] % 2
        prep_cnt[0] += 1
        xa, xs = xa_bufs[s], xs_bufs[s]
        ra, rs = "%sxa%d" % (tag, s), "%sxs%d" % (tag, s)
        if do_load:
            dma("sp", xlf[s], xa, src_ap, [], [ra], extra=extra)
        ss, ssr = statcol()
        act(xs, xa, AF.Square, [ra], [rs, ssr], accum=ss, extra=extra)
        r, rres = (rstd_fn or rstd_pool)(ss, ssr, 1.0 / D, extra=extra)
        yield
        stt(xs, xa, r, gb, ALU.mult, ALU.mult, [ra, rres, gbres], [rs], extra=extra)
        yield
        for k in range(KD):
            tr(psT[:, k * P:(k + 1) * P], xs[:, k * P:(k + 1) * P], [rs], ["psT"], extra=extra)
        act(dstT[:, :, col0:col0 + P], psT.rearrange("p (k t) -> p k t", t=P), AF.Copy, ["psT"], [dst_res], extra=extra)
        yield

    def prep_tile(*a, **kw):
        for _ in prep_tile_gen(*a, **kw):
            pass

    def m_load(t):
        s = t % 3
        dma("sp", xl[s], m_xa[s], xall[t * P:(t + 1) * P, :], [], ["mxa%d" % s])

    def m_norm(t, hp, col0, evac_dve=False):
        s = t % 3
        xa, ra = m_xa[s], "mxa%d" % s
        xs, rs = m_xs[t % 2], "mxs%d" % (t % 2)
        ss, ssr = statcol()
        act(xs, xa, AF.Square, [ra], [rs, ssr], accum=ss)
        r, rres = rstd_from_ss(ss, ssr, 1.0 / D)
        yield
        stt(xs, xa, r, gb_pre_mix, ALU.mult, ALU.mult, [ra, rres, "gb_pre_mix"], [rs])
        if t + 3 < NT_ALL:
            m_load(t + 3)
        yield
        for k in range(KD):
            tr(psT[:, k * P:(k + 1) * P], xs[:, k * P:(k + 1) * P], [rs], ["psT"])
        if evac_dve:
            S.add("dve", lambda e: e.tensor_copy(out=hT2[hp][:, :, col0:col0 + P],
                                                   in_=psT.rearrange("p (k t) -> p k t", t=P)),
                  ["psT"], ["hT%d" % hp], cost=0.12 + 1024 / 900.0)
        else:
            act(hT2[hp][:, :, col0:col0 + P], psT.rearrange("p (k t) -> p k t", t=P), AF.Copy, ["psT"], ["hT%d" % hp])
        yield

    state = {"cur": 0, "first": True}

    def make_macro(tiles, full, par):
        nt = len(tiles)
        T = nt * P
        hT = hT2[par]
        RH = "hT%d" % par
        Vt_, ut_, Qz_, Kt_, Ktt_, sr_, dec_ = Vt[par], ut[par], Qz[par], Kt[par], Ktt[par], sr[par], dec[par]
        sx = "_%d" % par
        RV = lambda i: "Vt%d%s" % (i, sx)
        RU = lambda i: "ut%d%s" % (i, sx)
        RQ = lambda h: "Qz%d%s" % (h, sx)
        RK = lambda c: "Kt%d%s" % (c, sx)
        RKT = lambda i: "Ktt%d%s" % (i, sx)
        RS = lambda h: "sr%d%s" % (h, sx)
        RD = "dec" + sx

        def prep_gen():
            gens = [m_norm(tiles[i], par, i * P, evac_dve=not full) for i in range(nt)]
            live = []
            nx = 0
            while nx < nt or live:
                if nx < nt and len(live) < 2:
                    live.append(gens[nx])
                    nx += 1
                for g_ in list(live):
                    try:
                        next(g_)
                    except StopIteration:
                        live.remove(g_)
                yield

        def proj_fm(c0, ncols, wres):
            b = bank()
            for k in range(KD):
                mm(psum[0:ncols, b, 0:T], w_in_sb[:, k, c0:c0 + ncols], hT[:, k, 0:T], k == 0, k == KD - 1,
                   [RH] + wres, [PS(b)], wfuse=True)
            return b

        def vproj(i):
            b = bank()
            for k in range(KD):
                mm(psum[:, b, :], hT[:, k, i * P:(i + 1) * P], w_in_sb[:, k, C_V:C_V + 512], k == 0, k == KD - 1,
                   [RH] + W_IN_A, [PS(b)])
            act(Vt_[:, i, :], psum[:, b, :], AF.Copy, [PS(b)], [RV(i)])
            rel(b)

        def a2a_gen():
            bg = proj_fm(C_G, P, W_IN_A + W_IN_C)
            act(glT[:, 0:T], psum[0:16, bg, 0:T], AF.Copy, [PS(bg)], ["glT"])
            rel(bg)
            yield
            for c in range(2):
                b = bank()
                S.add("pe", lambda e, b=b, c=c: e.matmul(psum[:, b, 0:T], w_gu_sb[:, c * P:(c + 1) * P], glT[:, 0:T],
                                                           start=True, stop=True),
                      ["glT", "w_gu"], [PS(b)])
                act(lg[:, c, 0:T], psum[:, b, 0:T], AF.Exp, [PS(b), "negb"], ["lg%d" % c], bias=negb[:, c:c + 1],
                    scale=-1.0)
                rel(b)
                act(lg[:, c, 0:T], lg[:, c, 0:T], AF.Ln, ["lg%d" % c], ["lg%d" % c], bias=1.0, scale=1.0)
                yield
                for i in range(nt):
                    S.add("dve", lambda e, c=c, i=i: e.tensor_tensor_scan(
                        out=braw[:, c, i * P:(i + 1) * P], data0=ones_f[:, :], data1=lg[:, c, i * P:(i + 1) * P],
                        initial=0.0, op0=ALU.mult, op1=ALU.add), ["lg%d" % c, "ones_f"], ["braw%d" % c])
                yield
                act(Ek[:, c, 0:T], braw[:, c, 0:T], AF.Exp, ["braw%d" % c], ["Ek%d" % c], scale=1.0 / 16.0)
                if full:
                    act(lg[:, c, 0:T], braw[:, c, 0:T], AF.Exp, ["braw%d" % c], ["lg%d" % c],
                        bias=float(np.log(0.125)), scale=-1.0 / 16.0)
                yield
            act(dec_[:, :, 0:nt], braw[:, :, 0:T].rearrange("p c (i t) -> p c i t", t=P)[:, :, :, P - 1], AF.Exp,
                ["braw0", "braw1"], [RD], scale=-1.0 / 16.0)
            for c in range(2):
                bk = proj_fm(C_K + c * P, P, W_IN_A)
                tt(Kt_[:, c, 0:T], psum[:, bk, 0:T], Ek[:, c, 0:T], ALU.mult, [PS(bk), "Ek%d" % c], [RK(c)])
                rel(bk)
                yield
                if full:
                    bq = proj_fm(C_Q + c * P, P, W_IN_B)
                    for e_ in range(2):
                        rows = slice(e_ * 64, (e_ + 1) * 64)
                        tt(Qz_[rows, 2 * c + e_, 0:T], psum[rows, bq, 0:T], lg[rows, c, 0:T], ALU.mult,
                           [PS(bq), "lg%d" % c], [RQ(2 * c + e_)])
                    rel(bq)
                    yield
            for i in range(nt):
                for c in range(2):
                    tr(psT[:, c * P:(c + 1) * P], Kt_[:, c, i * P:(i + 1) * P], [RK(c)], ["psT"])
                act(Ktt_[:, i, :], psT[:, 0:2 * P], AF.Copy, ["psT"], [RKT(i)])
                yield

        def a2b_gen():
            for i in range(nt):
                vproj(i)
                yield
            if full:
                for i in range(nt):
                    b = bank()
                    for k in range(KD):
                        mm(psum[:, b, :], hT[:, k, i * P:(i + 1) * P], w_in_sb[:, k, C_U:C_U + 512], k == 0, k == KD - 1,
                           [RH] + W_IN_B, [PS(b)])
                    act(ut_[:, 1 + i, :], psum[:, b, :], AF.Copy, [PS(b)], [RU(1 + i)])
                    rel(b)
                    yield
                for h in range(4):
                    b = proj_fm(C_R + h * P, P, W_IN_C)
                    se, ser = setmp[h % 2][:, 0:T], "se%d" % (h % 2)
                    act(se, psum[:, b, 0:T], AF.Exp, [PS(b)], [ser], scale=-1.0)
                    act(se, se, AF.Ln, [ser], [ser], bias=1.0, scale=1.0)
                    act(se, se, AF.Exp, [ser], [ser], scale=-1.0)
                    tt(sr_[:, h, 0:T], psum[:, b, 0:T], se, ALU.mult, [PS(b), ser], [RS(h)])
                    rel(b)
                    yield

        def chunk(i):
            t = tiles[i]
            seg = slice(i * P, (i + 1) * P)
            cur = state["cur"]
            nxt = 1 - cur
            first = state["first"]
            if full:
                s1 = (t - HALO_T) % 2
                x1b, x1r = m_x1[s1], "mx1%d" % s1
                bA = bank()
                for h in range(4):
                    mm(psum[:, bA, h * P:(h + 1) * P], Kt_[:, h // 2, seg], Qz_[:, h, seg], True, True,
                       [RK(h // 2), RQ(h)], [PS(bA)])
                bd = bank()
                mcur = pm_sb[:, 2 if t == 16 else 0, :]
                mprev = pm_sb[:, 1, :]
                for g in range(4):
                    mm(psum[:, bd, g * P:(g + 1) * P], ut_[:, 1 + i, g * P:(g + 1) * P], mcur[:, g * P:(g + 1) * P], True,
                       False, [RU(1 + i), "pm0", "pm2"], [PS(bd)])
                    mm(psum[:, bd, g * P:(g + 1) * P], ut_[:, i, g * P:(g + 1) * P], mprev[:, g * P:(g + 1) * P], False,
                       True, [RU(i), "pm1"], [PS(bd)])
                yield
                AT = ATb[i % 2]
                ar = "AT%d" % (i % 2)
                tt(AT, psum[:, bA, :].rearrange("p (h t) -> p h t", t=P), trim, ALU.mult,
                   [PS(bA)] + ["trim%d" % h for h in range(4)], [ar])
                rel(bA)
                act(dT, psum[:, bd, :].rearrange("p (g t) -> p g t", t=P), AF.Copy, [PS(bd)], ["dT"])
                rel(bd)
                yield
                bO = bank()
                for h in range(4):
                    o_ap = psum[:, bO, h * P:(h + 1) * P]
                    mm(o_ap, Vt_[:, i, h * P:(h + 1) * P], AT[:, h, :], True, first, [RV(i), ar], [PS(bO)])
                    if not first:
                        mm(o_ap, Sb[cur][:, h // 2, :], Qz_[:, h, seg], False, True, ["Sb%d" % cur, RQ(h)], [PS(bO)])
                by = bank()
                for g in range(4):
                    mm(psum[:, by, g * P:(g + 1) * P], w_pool_sb[:, g, :], dT[:, g, :], True, True, ["dT", "w_pool"],
                       [PS(by)])
            bs = bank()
            for h in range(4):
                c, po = h // 2, (h % 2) * 64
                mm(psum[po:po + 64, bs, c * P:(c + 1) * P], Ktt_[:, i, h * 64:(h + 1) * 64], Vt_[:, i, h * P:(h + 1) * P],
                   True, True, [RKT(i), RV(i)], [PS(bs)])
            yield
            for c in range(2):
                dcol = dec_[:, c, i:i + 1]
                if first:
                    ts(Sst[nxt][:, c, :], psum[:, bs, c * P:(c + 1) * P], dcol, None, ALU.mult, ALU.bypass,
                       [PS(bs), RD], ["S%d" % nxt])
                else:
                    ts(Sd[:, c, :], Sst[cur][:, c, :], dcol, None, ALU.mult, ALU.bypass, ["S%d" % cur, RD], ["Sd"])
                    stt(Sst[nxt][:, c, :], psum[:, bs, c * P:(c + 1) * P], dcol, Sd[:, c, :], ALU.mult, ALU.add,
                        [PS(bs), RD, "Sd"], ["S%d" % nxt])
            rel(bs)
            act(Sb[nxt], Sst[nxt], AF.Copy, ["S%d" % nxt], ["Sb%d" % nxt])
            state["cur"] = nxt
            state["first"] = False
            if not full:
                yield
                return
            ops_o = psum[:, bO, :].rearrange("p (h t) -> p h t", t=P)
            act(sq, ops_o, AF.Square, [PS(bO)], ["sq"])
            tt(yT[:, 0:4, seg], psum[:, by, :].rearrange("p (g t) -> p g t", t=P), psb, ALU.mult, [PS(by), "psb"], ["yTp%d" % i])
            rel(by)
            yield
            bb = bank()
            for h in range(4):
                mm(psum[:, bb, h * P:(h + 1) * P], ones_bf, sq[:, h, :], True, True, ["sq", "ones_bf"], [PS(bb)])
            yield
            act(rb, psum[:, bb, :].rearrange("p (h t) -> p h t", t=P), AF.Ln, [PS(bb)], ["rb"], bias=EPS, scale=1.0 / P)
            rel(bb)
            act(rb, rb, AF.Exp, ["rb"], ["rb"], scale=-0.5)
            yield
            dma("sp", x1l[s1], x1b, xall[t * P:(t + 1) * P, :], [], [x1r])
            tt(tb, ops_o, rb, ALU.mult, [PS(bO), "rb"], ["tb"])
            rel(bO)
            stt(yT[:, 4:8, seg], tb, c_ggla, sr_[:, :, seg], ALU.mult, ALU.mult,
                ["tb", "cols"] + [RS(h) for h in range(4)], ["yTg%d" % i])
            yield
            bm = bank_pair()
            for half in range(2):
                for k in range(KD):
                    mm(psum[:, bm + half, :], yT[:, k, seg], w_out_sb[:, k, half * 512:(half + 1) * 512], k == 0, k == KD - 1,
                       ["yTp%d" % i, "yTg%d" % i] + W_OUT, [PS(bm + half)])
            yield
            mixps = psum[:, bm:bm + 2, :].rearrange("p a n -> p (a n)")
            ss, ssr = statcol()
            act(m_t1, mixps, AF.Square, [PS(bm), PS(bm + 1)], ["m_t1", ssr], accum=ss)
            r, rres = rstd_from_ss(ss, ssr, 1.0 / D)
            tt(m_t1, mixps, gb_post_mix, ALU.mult, [PS(bm), PS(bm + 1), "gb_post_mix"], ["m_t1"])
            rel(bm, bm + 1)
            yield
            stt(x1b, m_t1, r, x1b, ALU.mult, ALU.add, ["m_t1", rres, x1r], [x1r])
            if t >= 16:
                dma("sp", x1l[s1], out[(t - 16) * P:(t - 15) * P, :], x1b, [x1r], ["out%d" % (t - 16)])
            else:
                ss2, ss2r = statcol()
                act(hx, x1b, AF.Square, [x1r], ["hx", ss2r], accum=ss2)
                r2, r2res = rstd_from_ss(ss2, ss2r, 1.0 / D)
                stt(hx, x1b, r2, gb_pre_ffn, ALU.mult, ALU.mult, [x1r, r2res, "gb_pre_ffn"], ["hx"])
                for k in range(KD):
                    tr(psT[:, k * P:(k + 1) * P], hx[:, k * P:(k + 1) * P], ["hx"], ["psT"])
                act(hh, psT.rearrange("p (k t) -> p k t", t=P)[:, :, P - 2:P], AF.Copy, ["psT"], ["hh"])
            if i == nt - 1 and tiles[-1] != NT_ALL - 1:
                S.add("dve", lambda e: e.tensor_copy(out=ut[1 - par][:, 0, :], in_=ut_[:, nt, :]), [RU(nt)],
                      ["ut0_%d" % (1 - par)])
            yield

        def b_gen():
            lag = 4 if full else 2
            gens = [chunk(i) for i in range(nt)]
            active = []
            nxt_i = 0
            while nxt_i < nt or active:
                if nxt_i < nt and len(active) < 3 and (not active or active[-1][1] >= lag):
                    active.append([gens[nxt_i], 0])
                    nxt_i += 1
                for a_ in list(active):
                    try:
                        next(a_[0])
                        a_[1] += 1
                    except StopIteration:
                        active.remove(a_)
                yield

        n_a2a = 1 + 6 + (4 if full else 2) + nt
        n_a2b = nt + ((nt + 4) if full else 0)
        n_b = (11 + 5 * (nt - 1)) if full else (3 * nt)
        return (prep_gen, 2 * nt + 3), (a2a_gen, n_a2a), (b_gen, n_b), (a2b_gen, n_a2b)

    step_no = [0]

    def run_merged(items, tags, scale=None):
        sid = step_no[0]
        step_no[0] += 1
        gens = [g() for g, _ in items]
        alive = [True] * len(gens)

        def advance(k):
            S.tag = tags[k]
            S.group = []
            try:
                next(gens[k])
                ok = True
            except StopIteration:
                alive[k] = False
                ok = False
            grp, S.group, S.tag = S.group, None, None
            return ok, grp

        if plan is not None:
            for k in plan[sid]:
                if alive[k]:
                    advance(k)
            for k in range(len(gens)):
                while alive[k]:
                    advance(k)
            return
        lens = [max(1, n) for _, n in items]
        if scale:
            lens = [l * scale.get(t, 1.0) for l, t in zip(lens, tags)]
        prog = [0] * len(gens)
        rec = [[] for _ in gens]
        while any(alive):
            k = min((j for j in range(len(gens)) if alive[j]), key=lambda j: (prog[j] + 0.5) / lens[j])
            ok, grp = advance(k)
            if ok:
                prog[k] += 1
                rec[k].append(grp)
        if record is not None:
            record.append((rec, list(tags)))

    def ffn_prep0(fence):
        base = prep_cnt[0]
        pg = [prep_tile_gen(out[i * P:(i + 1) * P, :], f_xa, f_xs, gb_pre_ffn, "gb_pre_ffn", h2T, "h2T", i * P, "f",
                            extra=fence, rstd_fn=rstd_from_ss, do_load=False) for i in range(4)]
        prog = [0] * 4
        live = []
        nx = 0
        while nx < 4 or live:
            if nx < 4 and len(live) < 2:
                live.append(nx)
                nx += 1
            for i in list(live):
                try:
                    next(pg[i])
                    prog[i] += 1
                    if prog[i] == 2 and i + 2 < 4:
                        s_ = (base + i) % 2
                        dma("sp", xlf[s_], f_xa[s_], out[(i + 2) * P:(i + 3) * P, :], [], ["fxa%d" % s_], extra=fence)
                except StopIteration:
                    live.remove(i)
            yield

    for par in range(2):
        S.add("dve", lambda e, par=par: e.memset(ut[par][:, 0, :], 0.0), [], ["ut0_%d" % par])
        for h_ in range(4):
            rows = slice(64, 128) if h_ % 2 == 0 else slice(0, 64)
            S.add("dve", lambda e, par=par, h_=h_, rows=rows: e.memset(Qz[par][rows, h_, :], 0.0), [],
                  ["Qzz%d_%d" % (h_, par)])
    load_class(0)
    macros = []
    pref = list(range(0, HALO_T))
    if "no_prefix" not in opts:
        for m0 in range(0, len(pref), 4):
            macros.append((pref[m0:m0 + 4], False))
    for tl in ([15, 16, 17, 18], [19, 20, 21, 22], [23, 24, 25, 26], [27, 28, 29, 30], [31]):
        macros.append((tl, "no_full" not in opts))
    built = [make_macro(tl, fl, idx % 2) for idx, (tl, fl) in enumerate(macros)]
    nM = len(built)
    for t0_ in range(3):
        m_load(t0_)
    dma("sp", newlane("c"), gb_post_mix, gbs[1], [], ["gb_post_mix"])
    dma("sp", newlane("c"), gb_pre_ffn, gbs[2], [], ["gb_pre_ffn"])
    dma("sp", newlane("c"), gb_post_ffn, gbs[3], [], ["gb_post_ffn"])
    dma("sp", newlane("c"), psb, psb_d, [], ["psb"])
    for s in range(-2, nM):
        items, tags = [], []
        if 0 <= s + 2 < nM:
            items.append(built[s + 2][0])
            tags.append("prep")
        if 0 <= s + 1 < nM:
            items.append(built[s + 1][1])
            tags.append("a2")
            items.append(built[s + 1][3])
            tags.append("a2")
        if 0 <= s < nM:
            items.append(built[s][2])
            tags.append("b")
        if s == nM - 1 and not stop_after_mixer:
            assert (nM - 1) % 2 == 0
            for i_ in range(2):
                s_ = (prep_cnt[0] + i_) % 2
                dma("sp", xlf[s_], f_xa[s_], out[i_ * P:(i_ + 1) * P, :], [], ["fxa%d" % s_], extra=fenceP)
            items.append((lambda: ffn_prep0(fenceP), 12))
            tags.append("fprep")
        run_merged(items, tags, scale={"a2": 3.0} if s == nM - 2 else None)
        if s == nM - 2:
            fenceP = S.fence()
            fenceA = S.fence_tags(["prep", "a2"])
            load_class(1, extra=fenceA)
    fenceM = S.fence()

    def fblock_of(f):
        for bi, (f0, nf, g, v, o, _) in enumerate(fblk):
            if f0 <= f < f0 + nf:
                return bi, f - f0
        raise AssertionError

    X = fenceM
    def f_prep(m, i):
        prep_tile(out[(4 * m + i) * P:(4 * m + i + 1) * P, :], f_xa, f_xs, gb_pre_ffn, "gb_pre_ffn", h2T, "h2T", i * P,
                  "f", extra=X)

    load_class(2, extra=fenceM)
    for m in range(0 if stop_after_mixer else 4):
        halc, haln = hal[m % 2], hal[(m + 1) % 2]
        hcr, hnr = "hal%d" % (m % 2), "hal%d" % ((m + 1) % 2)
        bh = None
        if m == 0:
            bh = bank()
        for f in range(NF):
            bi, fi = fblock_of(f)
            _, _, gw, vw, ow, _ = fblk[bi]
            bgt, bvl = bank(), bank()
            for k in range(KD):
                mm(psum[:, bgt, :], gw[:, k, fi * P:(fi + 1) * P], h2T[:, k, :], k == 0, k == KD - 1,
                   ["h2T"] + FWG(bi), [PS(bgt)], extra=X, wfuse=True)
            if m == 0:
                for k in range(KD):
                    mm(psum[:, bh, 2 * f:2 * f + 2], gw[:, k, fi * P:(fi + 1) * P], hh[:, k, :], k == 0, k == KD - 1,
                       ["hh"] + FWG(bi), [PS(bh)], extra=X)
                act(halc[:, f, :], psum[:, bh, 2 * f:2 * f + 2], AF.Copy, [PS(bh)], [hcr], extra=X)
            for k in range(KD):
                mm(psum[:, bvl, :], vw[:, k, fi * P:(fi + 1) * P], h2T[:, k, :], k == 0, k == KD - 1,
                   ["h2T"] + FWV(bi), [PS(bvl)], extra=X, wfuse=True)
            acc, accr = accb[f % 2], "acc%d" % (f % 2)
            gl, glr = glb[f % 2], "gl%d" % (f % 2)
            gps = psum[:, bgt, :]
            act(acc, gps, AF.Identity, [PS(bgt), "cols"], [accr], bias=c_cb[:, f:f + 1], scale=c_cw[:, f, 2:3], extra=X)
            act(haln[:, f, :], gps[:, 510:512], AF.Copy, [PS(bgt)], [hnr], extra=X)
            stt(acc[:, 1:512], gps[:, 0:511], c_cw[:, f, 1:2], acc[:, 1:512], ALU.mult, ALU.add, [PS(bgt), "cols", accr],
                [accr], extra=X)
            stt(acc[:, 2:512], gps[:, 0:510], c_cw[:, f, 0:1], acc[:, 2:512], ALU.mult, ALU.add, [PS(bgt), "cols", accr],
                [accr], extra=X)
            rel(bgt)
            stt(acc[:, 0:1], halc[:, f, 1:2], c_cw[:, f, 1:2], acc[:, 0:1], ALU.mult, ALU.add, [hcr, "cols", accr],
                [accr], extra=X)
            stt(acc[:, 0:2], halc[:, f, 0:2], c_cw[:, f, 0:1], acc[:, 0:2], ALU.mult, ALU.add, [hcr, "cols", accr],
                [accr], extra=X)
            act(gl, acc, AF.Gelu, [accr], [glr], extra=X)
            tt(actb[:, f, :], gl, psum[:, bvl, :], ALU.mult, [glr, PS(bvl)], ["actb%d" % f], extra=X)
            rel(bvl)
        if bh is not None:
            rel(bh)
        def f_load(i_):
            s_ = (prep_cnt[0] + (i_ - f_load.base)) % 2
            dma("sp", xlf[s_], f_xa[s_], out[(4 * m + 4 + i_) * P:(4 * m + 5 + i_) * P, :], [], ["fxa%d" % s_], extra=X)

        if m < 3:
            f_load.base = 0
            f_load(0)
            f_load(1)
        for i in range(4):
            ti = 4 * m + i
            pgen = None
            if m < 3:
                pgen = prep_tile_gen(out[(4 * m + 4 + i) * P:(4 * m + 5 + i) * P, :], f_xa, f_xs, gb_pre_ffn, "gb_pre_ffn",
                                     h2T, "h2T", i * P, "f", extra=X, do_load=False)
                next(pgen)
                next(pgen)
                if i + 2 < 4:
                    f_load.base = i + 1
                    f_load(i + 2)
            bo_ = bank_pair()
            for f in range(NF):
                bi, fi = fblock_of(f)
                ow = fblk[bi][4]
                for half in range(2):
                    mm(psum[:, bo_ + half, :], actb[:, f, i * P:(i + 1) * P], ow[:, fi, half * 512:(half + 1) * 512],
                       f == 0, f == NF - 1, ["actb%d" % f, "fwo%d" % bi], [PS(bo_ + half)], extra=X)
            if pgen is not None:
                for _ in pgen:
                    pass
            ffps = psum[:, bo_:bo_ + 2, :].rearrange("p a n -> p (a n)")
            s1 = ti % 2
            x1b, x1r = f_x1[s1], "fx1%d" % s1
            dma("sp", x1l[s1], x1b, out[ti * P:(ti + 1) * P, :], ["out%d" % ti], [x1r], extra=X)
            ss, ssr = statcol()
            act(f_t1, ffps, AF.Square, [PS(bo_), PS(bo_ + 1)], ["f_t1", ssr], accum=ss, extra=X)
            r, rres = rstd_pool(ss, ssr, 1.0 / D, extra=X)
            tt(f_t1, ffps, gb_post_ffn, ALU.mult, [PS(bo_), PS(bo_ + 1), "gb_post_ffn"], ["f_t1"], extra=X)
            rel(bo_, bo_ + 1)
            stt(x1b, f_t1, r, x1b, ALU.mult, ALU.add, ["f_t1", rres, x1r], [x1r], extra=X)
            dma("sp", x1l[s1], out[ti * P:(ti + 1) * P, :], x1b, [x1r], ["out%d" % ti], extra=X)
    final_deps = S.fence()
    S.add("sp", lambda e: e.nop(), extra=final_deps)

    S.finalize()
    sems = {}
    for e in ("pe", "act", "dve", "pool"):
        sems[e] = es.enter_context(nc.semaphore("s_" + e))
    for lane in S.lane_cnt:
        sems[("L", lane)] = es.enter_context(nc.semaphore("l_" + lane))
    with nc.Block() as block:
        @block.tensor
        def _(e):
            S.emit_engine("pe", e, sems)

        @block.scalar
        def _(e):
            S.emit_engine("act", e, sems)

        @block.vector
        def _(e):
            S.emit_engine("dve", e, sems)

        @block.gpsimd
        def _(e):
            S.emit_engine("pool", e, sems)

        @block.sync
        def _(e):
            S.emit_engine("sp", e, sems)
    es.close()
    build_program.info = dict(M_BASE=M_BASE, A_END=A_END, M_USED=M_USED, gv_class=gv_class, wo_class=wo_class)
    return nc


def make_plan(record, slack=1.0, rng=None, noise=0.0):
    eng_free = {}
    finish = {}
    plan = []
    nsteps = len(record)
    owner = {}
    for sid_, (rec_, tags_) in enumerate(record):
        for k_, stream in enumerate(rec_):
            for grp in stream:
                for op in grp:
                    owner[op[0]] = (sid_, k_)

    op_eng = {}

    def dep_time(deps, sid, k, eng):
        t = 0.0
        for d in deps:
            o = owner.get(d)
            if o is not None and o[0] == sid and o[1] != k:
                continue
            if d in finish:
                t = max(t, finish[d] + (0.0 if op_eng.get(d) == eng else 0.8))
        return t

    def chain_cost(grp):
        per = {}
        for (idx, eng, cost, deps, is_dma) in grp:
            per[eng] = per.get(eng, 0.0) + (0.06 if is_dma else cost)
        return (max(per.values()) if per else 0.0) + 0.4

    for sid, (rec, tags) in enumerate(record):
        ptr = [0] * len(rec)
        order = []
        remaining = [sum(chain_cost(g) for g in r) for r in rec]
        boost = {k: (1e6 if (sid == nsteps - 2 and tags[k] == "a2") else 0.0) for k in range(len(rec))}
        while True:
            cands = [k for k in range(len(rec)) if ptr[k] < len(rec[k])]
            if not cands:
                break
            sts = {}
            for k in cands:
                grp = rec[k][ptr[k]]
                if not grp:
                    sts[k] = -1e9
                else:
                    idx, eng, cost, deps, is_dma = grp[0]
                    sts[k] = max(eng_free.get(eng, 0.0), dep_time(deps, sid, k, eng))
            tmin = min(sts.values())
            near = [k for k in cands if sts[k] <= tmin + slack]
            if rng is not None:
                k = max(near, key=lambda j: (remaining[j] * (1.0 + noise * (2.0 * rng.random() - 1.0)) + boost[j], -j))
            else:
                k = max(near, key=lambda j: (remaining[j] + boost[j], -j))
            for (idx, eng, cost, deps, is_dma) in rec[k][ptr[k]]:
                st = max(eng_free.get(eng, 0.0), dep_time(deps, sid, k, eng))
                op_eng[idx] = "dma" if is_dma else eng
                if is_dma:
                    eng_free[eng] = st + 0.06
                    finish[idx] = st + cost
                else:
                    eng_free[eng] = st + cost
                    finish[idx] = st + cost
            remaining[k] -= chain_cost(rec[k][ptr[k]])
            ptr[k] += 1
            order.append(k)
        plan.append(order)
        make_plan.step_ends = getattr(make_plan, "step_ends", [])
        if sid == 0:
            make_plan.step_ends = []
        make_plan.step_ends.append(dict(eng_free))
    make_plan.model_time = max(eng_free.values()) if eng_free else 0.0
    return plan


def search_plan(record, trials=60):
    import random
    best = make_plan(record)
    best_t = make_plan.model_time
    rng = random.Random(1234)
    for i in range(trials):
        sl = (0.5, 1.0, 1.5, 2.0)[i % 4]
        p = make_plan(record, slack=sl, rng=rng, noise=0.35)
        if make_plan.model_time < best_t:
            best, best_t = p, make_plan.model_time
    search_plan.model_time = best_t
    tags = record[-1][1]
    if len(tags) == 2 and tags[1] == "fprep":
        n0, n1 = (len(s) for s in record[-1][0])
        best = list(best[:-1]) + [[0] * n0 + [1] * n1]
    return best


def _pool_mats(first_half):
    wins = (2, 4, 8, 16)
    mats = np.zeros((3, 4, P, P), np.float32)
    j = np.arange(P)[:, None]
    i = np.arange(P)[None, :]
    for g, w in enumerate(wins):
        band = ((j <= i) & (j > i - w)).astype(np.float32)
        mats[0, g] = band / w - np.eye(P, dtype=np.float32)
        mats[1, g] = ((j + 0 > P + i - w)).astype(np.float32) / w
        if first_half:
            cnt = np.minimum(np.arange(1, P + 1), w).astype(np.float32)[None, :]
            mats[2, g] = band / cnt - np.eye(P, dtype=np.float32)
        else:
            mats[2, g] = mats[0, g]
    return mats


_NC_CACHE = {}


def kernel(x, g_pre_mix, w_in, w_pool, pool_scale, w_gate_up, b_gate, g_gla_norm, w_out,
           g_post_mix, g_pre_ffn, w_ffn_in, conv_w, conv_b, w_ffn_out, g_post_ffn):
    f = lambda a: np.ascontiguousarray(np.asarray(a, dtype=np.float32))
    x = f(x)
    B, SEQ, _ = x.shape
    gbs = np.stack([np.broadcast_to(f(g)[None, :], (P, D)) for g in (g_pre_mix, g_post_mix, g_pre_ffn, g_post_ffn)])
    cols = np.zeros((P, NCOL), np.float32)
    cols[:, 0:2] = f(b_gate).reshape(2, P).T
    cols[:, 2] = f(g_gla_norm)
    cols[:, 3:7] = f(pool_scale).reshape(4, P).T
    cw = f(conv_w)
    cols[:, 7:73] = cw.reshape(3, NF, P).transpose(2, 1, 0).reshape(P, NF * 3)
    cols[:, 73:95] = f(conv_b).reshape(NF, P).T
    cmask = np.zeros((P, 3, P), np.float32)
    cmask[:, 0, :] = np.eye(P)
    cmask[:, 1, :] = (np.arange(P)[:, None] <= np.arange(P)[None, :])
    cmask[:, 2, :] = 1.0
    psb = np.ascontiguousarray(np.broadcast_to(f(pool_scale).reshape(4, P).T[:, :, None], (P, 4, P)))
    shared = {
        "w_in": f(w_in), "w_out": f(w_out), "w_ffn_in": f(w_ffn_in), "w_ffn_out": f(w_ffn_out),
        "w_pool": f(w_pool), "w_gu": f(w_gate_up), "gbs": np.ascontiguousarray(gbs), "cols": cols,
        "cmask": cmask, "psb": psb,
    }
    pm = {True: _pool_mats(True), False: _pool_mats(False)}
    in_maps = []
    half = SEQ // 2
    for c in range(8):
        b, hf = c // 2, c % 2
        if hf == 0:
            xa = np.concatenate([np.zeros((half, D), np.float32), x[b, :half]], axis=0)
        else:
            xa = x[b]
        m = dict(shared)
        m["xall"] = np.ascontiguousarray(xa)
        m["pmats"] = pm[hf == 0]
        in_maps.append(m)
    if "nc" not in _NC_CACHE:
        rec = []
        build_program(record=rec)
        _NC_CACHE["nc"] = build_program(plan=search_plan(rec))
    res = run_bass_kernel_spmd(_NC_CACHE["nc"], in_maps, core_ids=list(range(8)))
    outp = np.empty((B, SEQ, D), np.float32)
    for c in range(8):
        b, hf = c // 2, c % 2
        outp[b, hf * half:(hf + 1) * half] = res.results[c]["out"]
    return outp
```

```python
import numpy as np
from contextlib import ExitStack
import concourse.bass as bass
import concourse.mybir as mybir
from concourse.bass_utils import run_bass_kernel_spmd

F32 = mybir.dt.float32
BF16 = mybir.dt.bfloat16
U8 = mybir.dt.uint8
AF = mybir.ActivationFunctionType
ALU = mybir.AluOpType

P = 128
D = 1024
KD = 8
DIN = 2064
DFF = 2816
NF = 22
NT_ALL = 32
HALO_T = 15
EPS = 1e-6
C_U, C_Q, C_K, C_V, C_G, C_R = 0, 512, 768, 1024, 1536, 1552
FBLOCKS = [(0, 2), (2, 4), (6, 4), (10, 4), (14, 4), (18, 4)]
NCOL = 2 + 1 + 4 + 66 + 22


class _Op:
    __slots__ = ("eng", "emit", "deps", "lane", "idx", "eidx", "signal", "sigcount", "lanecount", "waits", "fuse")


class Sched:
    ENG = ("pe", "act", "dve", "pool", "sp")

    def __init__(self):
        self.ops = []
        self.byeng = {e: [] for e in self.ENG}
        self.lastw = {}
        self.readers = {}
        self.lane_last = {}
        self.lane_cnt = {}
        self.dom_last = {}
        self.tag = None
        self.tag_last = {}
        self.group = None

    def fence_tags(self, tags):
        out = set()
        for t in tags:
            out.update(self.tag_last.get(t, {}).values())
        return out

    def _dom(self, op):
        return ("L", op.lane) if op.lane is not None else op.eng

    def add(self, eng, emit, reads=(), writes=(), lane=None, extra=(), cost=0.3, fuse=None):
        idx = len(self.ops)
        deps = set(extra)
        for r in reads:
            if r in self.lastw:
                deps.add(self.lastw[r])
        for w in writes:
            if w in self.lastw:
                deps.add(self.lastw[w])
            deps.update(self.readers.get(w, {}).values())
        if lane is not None and lane in self.lane_last:
            deps.add(self.lane_last[lane])
        deps.discard(idx)
        op = _Op()
        op.eng, op.emit, op.deps, op.lane, op.idx = eng, emit, deps, lane, idx
        op.signal, op.sigcount, op.lanecount, op.waits = False, 0, 0, None
        op.fuse = (eng in ("dve", "act") and lane is None) if fuse is None else fuse
        dom = ("L", lane) if lane is not None else eng
        for r in reads:
            self.readers.setdefault(r, {})[dom] = idx
        for w in writes:
            self.lastw[w] = idx
            self.readers[w] = {}
        if lane is not None:
            self.lane_last[lane] = idx
            self.lane_cnt[lane] = self.lane_cnt.get(lane, 0) + 16
            op.lanecount = self.lane_cnt[lane]
        op.eidx = len(self.byeng[eng])
        self.ops.append(op)
        self.byeng[eng].append(op)
        self.dom_last[dom] = idx
        self.tag_last.setdefault(self.tag, {})[dom] = idx
        if self.group is not None:
            self.group.append((idx, eng, cost, tuple(deps), lane is not None))
        return idx

    def fence(self):
        return set(self.dom_last.values())

    def _needed(self, x, d):
        if d.lane is not None:
            return True
        if d.eng == x.eng and x.lane is None:
            return x.eng != "pe"
        return True

    def finalize(self):
        ops = self.ops

        def dom(o):
            return ("L", o.lane) if o.lane is not None else o.eng

        know = {e: {} for e in self.ENG}
        after = [None] * len(ops)
        needed = [None] * len(ops)
        for x in ops:
            E = x.eng
            K = dict(know[E])
            nd = []
            for di in sorted(x.deps, reverse=True):
                d = ops[di]
                if d.lane is None and x.lane is None and d.eng == "pe" and E == "pe":
                    continue
                dm = dom(d)
                if K.get(dm, -1) >= di:
                    continue
                nd.append(di)
                for k2, v2 in after[di].items():
                    if K.get(k2, -1) < v2:
                        K[k2] = v2
            needed[x.idx] = nd
            know[E] = K
            a_ = dict(K)
            dx = dom(x)
            if a_.get(dx, -1) < x.idx:
                a_[dx] = x.idx
            after[x.idx] = a_
        for x in ops:
            for di in needed[x.idx]:
                if ops[di].lane is None:
                    ops[di].signal = True
        for e in self.ENG:
            c = 0
            for op in self.byeng[e]:
                if op.lane is None and op.signal:
                    c += 1
                op.sigcount = c
        for x in ops:
            w = {}
            for di in needed[x.idx]:
                d = ops[di]
                if d.lane is not None:
                    key, val = ("L", d.lane), d.lanecount
                else:
                    key, val = d.eng, d.sigcount
                if w.get(key, 0) < val:
                    w[key] = val
            x.waits = w

    def emit_engine(self, e, engine, sems):
        for x in self.byeng[e]:
            w = list(x.waits.items())
            fuse_ok = bool(x.fuse)
            if x.fuse == "w" and any(isinstance(k, tuple) for k, _ in w):
                fuse_ok = False
            fused = w.pop() if (w and fuse_ok) else None
            for k, v in w:
                engine.wait_ge(sems[k], v)
            ins = x.emit(engine)
            if fused is not None:
                ins._wait_ge(sems[fused[0]], fused[1])
            if x.lane is not None:
                ins.then_inc(sems[("L", x.lane)], 16)
            elif x.signal:
                ins.then_inc(sems[e], 1)


def build_program(stop_after_mixer=False, plan=None, record=None, opts=()):
    nc = bass.Bass("TRN2", target_bir_lowering=False)

    def din(name, shape, dt=F32):
        return nc.dram_tensor(name, list(shape), dt, kind="ExternalInput").ap()

    xall = din("xall", [NT_ALL * P, D])
    w_in = din("w_in", [D, DIN])
    w_out = din("w_out", [D, D])
    w_fi = din("w_ffn_in", [D, 2 * DFF])
    w_fo = din("w_ffn_out", [DFF, D])
    w_pool = din("w_pool", [4, P, P])
    w_gu = din("w_gu", [16, 256])
    gbs = din("gbs", [4, P, D])
    cols_d = din("cols", [P, NCOL])
    cmask = din("cmask", [P, 3, P])
    pmats = din("pmats", [3, 4, P, P])
    psb_d = din("psb", [P, 4, P])
    out = nc.dram_tensor("out", [16 * P, D], F32, kind="ExternalOutput").ap()

    S = Sched()
    TOTAL = 212480
    es = ExitStack()
    big = es.enter_context(nc.sbuf_tensor("big", [P, TOTAL], U8))
    psum = es.enter_context(nc.psum_tensor("ps", [P, 8, 512], F32))

    def carve(off, shape, dt):
        n = int(np.prod(shape[1:]))
        sz = n * (4 if dt == F32 else 2)
        ap = big[0:shape[0], off:off + sz].bitcast(dt)
        if len(shape) == 3:
            ap = ap.rearrange("p (a b) -> p a b", b=shape[2])
        elif len(shape) == 4:
            ap = ap.rearrange("p (a b c) -> p a b c", b=shape[2], c=shape[3])
        return ap

    class Arena:
        def __init__(self, base, limit):
            self.off, self.limit = base, limit

        def get(self, shape, dt):
            n = int(np.prod(shape[1:])) * (4 if dt == F32 else 2)
            n = (n + 63) // 64 * 64
            o = self.off
            self.off += n
            assert self.off <= self.limit, (self.off, self.limit)
            return carve(o, shape, dt)

    PERS = 10240
    pa = Arena(0, PERS)
    ident = pa.get([P, P], BF16)
    ones_bf = pa.get([P, P], BF16)
    gb_pre_ffn = pa.get([P, D], F32)
    gb_post_ffn = pa.get([P, D], F32)
    cols = pa.get([P, NCOL], F32)
    negb = pa.get([P, 2], F32)
    hal = [pa.get([P, NF, 2], F32) for _ in range(2)]
    hh = pa.get([P, KD, 2], BF16)
    stat = pa.get([P, 64], F32)
    mhalf = pa.get([P, 1], F32)
    c_bg = cols[:, 0:2]
    c_ggla = cols[:, 2:3]
    c_psc = cols[:, 3:7]
    c_cw = cols[:, 7:73].rearrange("p (f t) -> p f t", t=3)
    c_cb = cols[:, 73:95]

    FW0 = PERS
    fblk = []
    off = FW0
    gv_off = []
    for (f0, nf) in FBLOCKS:
        gv_off.append(off)
        off += 2 * KD * nf * P * 2
    wo_off = []
    for (f0, nf) in FBLOCKS:
        wo_off.append(off)
        off += nf * D * 2
    for bi, (f0, nf) in enumerate(FBLOCKS):
        g = carve(gv_off[bi], [P, KD, nf * P], BF16)
        v = carve(gv_off[bi] + KD * nf * P * 2, [P, KD, nf * P], BF16)
        o = carve(wo_off[bi], [P, nf, D], BF16)
        fblk.append((f0, nf, g, v, o, gv_off[bi]))
    FW_END = off
    assert FW_END == PERS + 135168

    M_SIZE = 193920
    M_BASE = TOTAL - M_SIZE
    ma = Arena(M_BASE, TOTAL)
    w_in_sb = ma.get([P, KD, DIN], BF16)
    hT2 = [ma.get([P, KD, 512], BF16) for _ in range(2)]
    m_xa = [ma.get([P, D], F32) for _ in range(3)]
    setmp = [ma.get([P, 512], F32) for _ in range(2)]
    m_xs = [ma.get([P, D], BF16) for _ in range(2)]
    glT = ma.get([16, 512], BF16)
    lg = ma.get([P, 2, 512], F32)
    braw = ma.get([P, 2, 512], F32)
    Ek = ma.get([P, 2, 512], F32)
    w_gu_sb = ma.get([16, 256], BF16)
    gb_pre_mix = ma.get([P, D], F32)
    ones_f = ma.get([P, P], F32)
    A_END = ma.off
    w_out_sb = ma.get([P, KD, D], BF16)
    w_pool_sb = ma.get([P, 4, P], BF16)
    gb_post_mix = ma.get([P, D], F32)
    trim = ma.get([P, 4, P], BF16)
    pm_sb = ma.get([P, 3, 4 * P], BF16)
    psb = ma.get([P, 4, P], F32)
    m_x1 = [ma.get([P, D], F32) for _ in range(2)]
    hx = ma.get([P, D], BF16)
    _shapes = (("ut", [P, 5, 512], BF16), ("Vt", [P, 4, 512], BF16), ("Qz", [P, 4, 512], BF16),
               ("Kt", [P, 2, 512], BF16), ("Ktt", [P, 4, 256], BF16), ("sr", [P, 4, 512], BF16), ("dec", [P, 2, 4], F32))
    _p0 = {n: ma.get(s, d) for n, s, d in _shapes}
    ATb = [ma.get([P, 4, P], BF16) for _ in range(2)]
    m_x1l = None
    sq = ma.get([P, 4, P], BF16)
    rb = ma.get([P, 4, P], F32)
    tb = ma.get([P, 4, P], F32)
    dT = ma.get([P, 4, P], BF16)
    yT = ma.get([P, KD, 512], BF16)
    m_t1 = ma.get([P, D], F32)
    Sst = [ma.get([P, 2, P], F32) for _ in range(2)]
    Sd = ma.get([P, 2, P], F32)
    Sb = [ma.get([P, 2, P], BF16) for _ in range(2)]
    P1_BASE = ma.off
    _p1 = {n: ma.get(s, d) for n, s, d in _shapes}
    P1_END = ma.off
    ut, Vt, Qz, Kt, Ktt, sr, dec = ([_p0[n], _p1[n]] for n in ("ut", "Vt", "Qz", "Kt", "Ktt", "sr", "dec"))
    def _cls(bend):
        return 0 if bend <= M_BASE else (1 if bend <= A_END else 2)
    gv_class = [_cls(gv_off[bi] + 2 * KD * nf * P * 2) for bi, (f0, nf) in enumerate(FBLOCKS)]
    wo_class = [_cls(wo_off[bi] + nf * D * 2) for bi, (f0, nf) in enumerate(FBLOCKS)]
    M_USED = ma.off

    fa = Arena(FW_END, P1_BASE)
    fp1 = Arena(P1_BASE, P1_END)
    actb = fa.get([P, NF, 512], BF16)
    f_xa = [fp1.get([P, D], F32) for _ in range(2)]
    f_x1 = [fa.get([P, D], F32) for _ in range(2)]
    f_xs = [fp1.get([P, D], BF16) for _ in range(2)]
    h2T = fp1.get([P, KD, 512], BF16)
    accb = [fa.get([P, 512], F32) for _ in range(2)]
    glb = [fa.get([P, 512], F32) for _ in range(2)]
    f_t1 = fa.get([P, D], F32)

    psT = psum[:, 7, :].bitcast(BF16)
    bank_rr = [0]
    held = set()

    def bank():
        for _ in range(8):
            b = bank_rr[0] % 7
            bank_rr[0] += 1
            if b not in held:
                held.add(b)
                return b
        raise AssertionError("no free PSUM bank: %r" % (held,))

    def bank_pair():
        for _ in range(16):
            b = bank_rr[0] % 7
            if b < 6 and b not in held and (b + 1) not in held:
                bank_rr[0] += 2
                held.add(b)
                held.add(b + 1)
                return b
            bank_rr[0] += 1
        raise AssertionError("no free PSUM bank pair: %r" % (held,))

    def rel(*bs):
        for b in bs:
            held.discard(b)

    def PS(b):
        return "ps%d" % b

    stat_rr = [0]

    def statcol(n=1):
        c = stat_rr[0] % 56
        if c + n > 56:
            c = 0
            stat_rr[0] = 0
        stat_rr[0] += n
        return stat[:, c:c + n], "stat%d" % c

    def _n(ap):
        return float(ap.free_size())

    def dma(eng, lane, out_ap, in_ap, reads, writes, extra=()):
        return S.add(eng, lambda e: e.dma_start(out=out_ap, in_=in_ap), reads, writes, lane=lane, extra=extra, cost=3.0)

    def act(out_ap, in_ap, func, reads, writes, bias=None, scale=None, accum=None, extra=()):
        kw = {}
        if bias is not None:
            kw["bias"] = bias
        if scale is not None:
            kw["scale"] = scale
        if accum is not None:
            kw["accum_out"] = accum
        return S.add("act", lambda e: e.activation(out=out_ap, in_=in_ap, func=func, **kw), reads, writes, extra=extra,
                     cost=0.25 + _n(in_ap) / 1200.0 + (0.1 if accum is not None else 0.0),
                     fuse=(accum is None))

    def tt(out_ap, a, b, op, reads, writes, extra=()):
        return S.add("dve", lambda e: e.tensor_tensor(out=out_ap, in0=a, in1=b, op=op), reads, writes, extra=extra,
                     cost=0.12 + _n(a) / 830.0)

    def ts(out_ap, a, s1, s2, op0, op1, reads, writes, extra=()):
        return S.add("dve", lambda e: e.tensor_scalar(out=out_ap, in0=a, scalar1=s1, scalar2=s2, op0=op0, op1=op1),
                     reads, writes, extra=extra, cost=0.12 + _n(a) / 900.0)

    def stt(out_ap, a, sc, b, op0, op1, reads, writes, extra=()):
        return S.add("dve", lambda e: e.scalar_tensor_tensor(out=out_ap, in0=a, scalar=sc, in1=b, op0=op0, op1=op1),
                     reads, writes, extra=extra, cost=0.12 + _n(a) / 830.0)

    def mm(out_ap, lhsT, rhs, start, stop, reads, writes, extra=(), wfuse=False):
        return S.add("pe", lambda e: e.matmul(out_ap, lhsT, rhs, start=start, stop=stop), reads, writes, extra=extra,
                     cost=0.04 + 0.00039 * _n(rhs), fuse=("w" if wfuse else False))

    def tr(out_ap, in_ap, reads, writes, extra=()):
        return S.add("pe", lambda e: e.transpose(out_ap, in_ap, ident), reads + ["ident"], writes, extra=extra, cost=0.11)

    def rstd_pool(ss_ap, ss_res, scale, extra=()):
        tmp, tres = statcol()
        S.add("pool", lambda e: e.tensor_scalar(out=tmp, in0=ss_ap, scalar1=scale, scalar2=EPS, op0=ALU.mult, op1=ALU.add),
              [ss_res], [tres], extra=extra)
        r, rres = statcol()
        S.add("pool", lambda e: e.tensor_tensor(out=r, in0=tmp, in1=mhalf, op=ALU.pow), [tres, "mhalf"], [rres], extra=extra)
        return r, rres

    def rstd_from_ss(ss_ap, ss_res, scale, extra=()):
        tmp, tres = statcol()
        act(tmp, ss_ap, AF.Ln, [ss_res], [tres], bias=EPS, scale=scale, extra=extra)
        r, rres = statcol()
        act(r, tmp, AF.Exp, [tres], [rres], scale=-0.5, extra=extra)
        return r, rres

    lane_id = [0]

    def newlane(prefix):
        lane_id[0] += 1
        return "%s%d" % (prefix, lane_id[0])

    LC = newlane("c")
    dma("sp", newlane("c"), cols, cols_d, [], ["cols"])
    dma("sp", newlane("c"), gb_pre_mix, gbs[0], [], ["gb_pre_mix"])
    wl = [newlane("w") for _ in range(8)]
    wl_i = [0]

    def wdma(out_ap, in_ap, writes, extra=()):
        lane = wl[wl_i[0] % len(wl)]
        wl_i[0] += 1
        return dma("pool", lane, out_ap, in_ap, [], writes, extra=extra)

    wdma(ident, cmask[:, 0, :], ["ident"])
    wdma(w_gu_sb, w_gu, ["w_gu"])
    for h_ in range(4):
        wdma(trim[:, h_, :], cmask[:, 1, :], ["trim%d" % h_])
    wdma(ones_bf, cmask[:, 2, :], ["ones_bf"])
    w_in_v = w_in.rearrange("(k p) n -> p k n", p=P)
    for (c0, c1, nm) in ((C_K, C_R, "w_in_a"), (C_R, DIN, "w_in_c"), (0, C_K, "w_in_b")):
        for k2 in range(0, KD, 2):
            wdma(w_in_sb[:, k2:k2 + 2, c0:c1], w_in_v[:, k2:k2 + 2, c0:c1], ["%s%d" % (nm, k2)])
    W_IN_A = ["w_in_a%d" % k for k in range(0, KD, 2)]
    W_IN_B = ["w_in_b%d" % k for k in range(0, KD, 2)]
    W_IN_C = ["w_in_c%d" % k for k in range(0, KD, 2)]
    S.add("dve", lambda e: e.memset(ones_f, 1.0), [], ["ones_f"])
    S.add("dve", lambda e: e.memset(mhalf, -0.5), [], ["mhalf"])
    ts(negb, c_bg, -1.0, None, ALU.mult, ALU.bypass, ["cols"], ["negb"])
    for g in range(3):
        wdma(pm_sb[:, g, :].rearrange("p (a b) -> p a b", b=P), pmats[g].rearrange("g j i -> j g i"), ["pm%d" % g])
    wdma(w_pool_sb, w_pool.rearrange("g c d -> c g d"), ["w_pool"])
    w_out_v = w_out.rearrange("(k p) n -> p k n", p=P)
    for k2 in range(0, KD, 4):
        wdma(w_out_sb[:, k2:k2 + 4, :], w_out_v[:, k2:k2 + 4, :], ["w_out%d" % k2])
    W_OUT = ["w_out0", "w_out4"]

    w_fi_v = w_fi.rearrange("(k p) n -> p k n", p=P)

    def load_gv(bi, extra=()):
        f0, nf, g, v, o, _ = fblk[bi]
        for k2 in range(0, KD, 4):
            wdma(g[:, k2:k2 + 4, :], w_fi_v[:, k2:k2 + 4, f0 * P:(f0 + nf) * P], ["fwg%d_%d" % (bi, k2)], extra=extra)
            wdma(v[:, k2:k2 + 4, :], w_fi_v[:, k2:k2 + 4, DFF + f0 * P:DFF + (f0 + nf) * P],
                 ["fwv%d_%d" % (bi, k2)], extra=extra)

    def load_wo(bi, extra=()):
        f0, nf, g, v, o, _ = fblk[bi]
        wdma(o, w_fo[f0 * P:(f0 + nf) * P, :].rearrange("(f p) n -> p f n", p=P), ["fwo%d" % bi], extra=extra)

    def load_class(cls, extra=()):
        for bi in range(len(fblk)):
            if gv_class[bi] == cls:
                load_gv(bi, extra=extra)
        for bi in range(len(fblk)):
            if wo_class[bi] == cls:
                load_wo(bi, extra=extra)

    def FWG(bi):
        return ["fwg%d_0" % bi, "fwg%d_4" % bi]

    def FWV(bi):
        return ["fwv%d_0" % bi, "fwv%d_4" % bi]

    xl = [newlane("x") for _ in range(4)]
    xlf = [newlane("xf") for _ in range(2)]
    x1l = [newlane("y") for _ in range(2)]
    prep_cnt = [0]

    def prep_tile_gen(src_ap, xa_bufs, xs_bufs, gb, gbres, dstT, dst_res, col0, tag, extra=(), rstd_fn=None,
                      do_load=True):
        s = prep_cnt[0] % 2
        prep_cnt[0] += 1
        xa, xs = xa_bufs[s], xs_bufs[s]
        ra, rs = "%sxa%d" % (tag, s), "%sxs%d" % (tag, s)
        if do_load:
            dma("sp", xlf[s], xa, src_ap, [], [ra], extra=extra)
        ss, ssr = statcol()
        act(xs, xa, AF.Square, [ra], [rs, ssr], accum=ss, extra=extra)
        r, rres = (rstd_fn or rstd_pool)(ss, ssr, 1.0 / D, extra=extra)
        yield
        stt(xs, xa, r, gb, ALU.mult, ALU.mult, [ra, rres, gbres], [rs], extra=extra)
        yield
        for k in range(KD):
            tr(psT[:, k * P:(k + 1) * P], xs[:, k * P:(k + 1) * P], [rs], ["psT"], extra=extra)
        act(dstT[:, :, col0:col0 + P], psT.rearrange("p (k t) -> p k t", t=P), AF.Copy, ["psT"], [dst_res], extra=extra)
        yield

    def prep_tile(*a, **kw):
        for _ in prep_tile_gen(*a, **kw):
            pass

    def m_load(t):
        s = t % 3
        dma("sp", xl[s], m_xa[s], xall[t * P:(t + 1) * P, :], [], ["mxa%d" % s])

    def m_norm(t, hp, col0, evac_dve=False):
        s = t % 3
        xa, ra = m_xa[s], "mxa%d" % s
        xs, rs = m_xs[t % 2], "mxs%d" % (t % 2)
        ss, ssr = statcol()
        act(xs, xa, AF.Square, [ra], [rs, ssr], accum=ss)
        r, rres = rstd_from_ss(ss, ssr, 1.0 / D)
        yield
        stt(xs, xa, r, gb_pre_mix, ALU.mult, ALU.mult, [ra, rres, "gb_pre_mix"], [rs])
        if t + 3 < NT_ALL:
            m_load(t + 3)
        yield
        for k in range(KD):
            tr(psT[:, k * P:(k + 1) * P], xs[:, k * P:(k + 1) * P], [rs], ["psT"])
        if evac_dve:
            S.add("dve", lambda e: e.tensor_copy(out=hT2[hp][:, :, col0:col0 + P],
                                                   in_=psT.rearrange("p (k t) -> p k t", t=P)),
                  ["psT"], ["hT%d" % hp], cost=0.12 + 1024 / 900.0)
        else:
            act(hT2[hp][:, :, col0:col0 + P], psT.rearrange("p (k t) -> p k t", t=P), AF.Copy, ["psT"], ["hT%d" % hp])
        yield

    state = {"cur": 0, "first": True}

    def make_macro(tiles, full, par):
        nt = len(tiles)
        T = nt * P
        hT = hT2[par]
        RH = "hT%d" % par
        Vt_, ut_, Qz_, Kt_, Ktt_, sr_, dec_ = Vt[par], ut[par], Qz[par], Kt[par], Ktt[par], sr[par], dec[par]
        sx = "_%d" % par
        RV = lambda i: "Vt%d%s" % (i, sx)
        RU = lambda i: "ut%d%s" % (i, sx)
        RQ = lambda h: "Qz%d%s" % (h, sx)
        RK = lambda c: "Kt%d%s" % (c, sx)
        RKT = lambda i: "Ktt%d%s" % (i, sx)
        RS = lambda h: "sr%d%s" % (h, sx)
        RD = "dec" + sx

        def prep_gen():
            gens = [m_norm(tiles[i], par, i * P, evac_dve=not full) for i in range(nt)]
            live = []
            nx = 0
            while nx < nt or live:
                if nx < nt and len(live) < 2:
                    live.append(gens[nx])
                    nx += 1
                for g_ in list(live):
                    try:
                        next(g_)
                    except StopIteration:
                        live.remove(g_)
                yield

        def proj_fm(c0, ncols, wres):
            b = bank()
            for k in range(KD):
                mm(psum[0:ncols, b, 0:T], w_in_sb[:, k, c0:c0 + ncols], hT[:, k, 0:T], k == 0, k == KD - 1,
                   [RH] + wres, [PS(b)], wfuse=True)
            return b

        def vproj(i):
            b = bank()
            for k in range(KD):
                mm(psum[:, b, :], hT[:, k, i * P:(i + 1) * P], w_in_sb[:, k, C_V:C_V + 512], k == 0, k == KD - 1,
                   [RH] + W_IN_A, [PS(b)])
            act(Vt_[:, i, :], psum[:, b, :], AF.Copy, [PS(b)], [RV(i)])
            rel(b)

        def a2a_gen():
            bg = proj_fm(C_G, P, W_IN_A + W_IN_C)
            act(glT[:, 0:T], psum[0:16, bg, 0:T], AF.Copy, [PS(bg)], ["glT"])
            rel(bg)
            yield
            for c in range(2):
                b = bank()
                S.add("pe", lambda e, b=b, c=c: e.matmul(psum[:, b, 0:T], w_gu_sb[:, c * P:(c + 1) * P], glT[:, 0:T],
                                                           start=True, stop=True),
                      ["glT", "w_gu"], [PS(b)])
                act(lg[:, c, 0:T], psum[:, b, 0:T], AF.Exp, [PS(b), "negb"], ["lg%d" % c], bias=negb[:, c:c + 1],
                    scale=-1.0)
                rel(b)
                act(lg[:, c, 0:T], lg[:, c, 0:T], AF.Ln, ["lg%d" % c], ["lg%d" % c], bias=1.0, scale=1.0)
                yield
                for i in range(nt):
                    S.add("dve", lambda e, c=c, i=i: e.tensor_tensor_scan(
                        out=braw[:, c, i * P:(i + 1) * P], data0=ones_f[:, :], data1=lg[:, c, i * P:(i + 1) * P],
                        initial=0.0, op0=ALU.mult, op1=ALU.add), ["lg%d" % c, "ones_f"], ["braw%d" % c])
                yield
                act(Ek[:, c, 0:T], braw[:, c, 0:T], AF.Exp, ["braw%d" % c], ["Ek%d" % c], scale=1.0 / 16.0)
                if full:
                    act(lg[:, c, 0:T], braw[:, c, 0:T], AF.Exp, ["braw%d" % c], ["lg%d" % c],
                        bias=float(np.log(0.125)), scale=-1.0 / 16.0)
                yield
            act(dec_[:, :, 0:nt], braw[:, :, 0:T].rearrange("p c (i t) -> p c i t", t=P)[:, :, :, P - 1], AF.Exp,
                ["braw0", "braw1"], [RD], scale=-1.0 / 16.0)
            for c in range(2):
                bk = proj_fm(C_K + c * P, P, W_IN_A)
                tt(Kt_[:, c, 0:T], psum[:, bk, 0:T], Ek[:, c, 0:T], ALU.mult, [PS(bk), "Ek%d" % c], [RK(c)])
                rel(bk)
                yield
                if full:
                    bq = proj_fm(C_Q + c * P, P, W_IN_B)
                    for e_ in range(2):
                        rows = slice(e_ * 64, (e_ + 1) * 64)
                        tt(Qz_[rows, 2 * c + e_, 0:T], psum[rows, bq, 0:T], lg[rows, c, 0:T], ALU.mult,
                           [PS(bq), "lg%d" % c], [RQ(2 * c + e_)])
                    rel(bq)
                    yield
            for i in range(nt):
                for c in range(2):
                    tr(psT[:, c * P:(c + 1) * P], Kt_[:, c, i * P:(i + 1) * P], [RK(c)], ["psT"])
                act(Ktt_[:, i, :], psT[:, 0:2 * P], AF.Copy, ["psT"], [RKT(i)])
                yield

        def a2b_gen():
            for i in range(nt):
                vproj(i)
                yield
            if full:
                for i in range(nt):
                    b = bank()
                    for k in range(KD):
                        mm(psum[:, b, :], hT[:, k, i * P:(i + 1) * P], w_in_sb[:, k, C_U:C_U + 512], k == 0, k == KD - 1,
                           [RH] + W_IN_B, [PS(b)])
                    act(ut_[:, 1 + i, :], psum[:, b, :], AF.Copy, [PS(b)], [RU(1 + i)])
                    rel(b)
                    yield
                for h in range(4):
                    b = proj_fm(C_R + h * P, P, W_IN_C)
                    se, ser = setmp[h % 2][:, 0:T], "se%d" % (h % 2)
                    act(se, psum[:, b, 0:T], AF.Exp, [PS(b)], [ser], scale=-1.0)
                    act(se, se, AF.Ln, [ser], [ser], bias=1.0, scale=1.0)
                    act(se, se, AF.Exp, [ser], [ser], scale=-1.0)
                    tt(sr_[:, h, 0:T], psum[:, b, 0:T], se, ALU.mult, [PS(b), ser], [RS(h)])
                    rel(b)
                    yield

        def chunk(i):
            t = tiles[i]
            seg = slice(i * P, (i + 1) * P)
            cur = state["cur"]
            nxt = 1 - cur
            first = state["first"]
            if full:
                s1 = (t - HALO_T) % 2
                x1b, x1r = m_x1[s1], "mx1%d" % s1
                bA = bank()
                for h in range(4):
                    mm(psum[:, bA, h * P:(h + 1) * P], Kt_[:, h // 2, seg], Qz_[:, h, seg], True, True,
                       [RK(h // 2), RQ(h)], [PS(bA)])
                bd = bank()
                mcur = pm_sb[:, 2 if t == 16 else 0, :]
                mprev = pm_sb[:, 1, :]
                for g in range(4):
                    mm(psum[:, bd, g * P:(g + 1) * P], ut_[:, 1 + i, g * P:(g + 1) * P], mcur[:, g * P:(g + 1) * P], True,
                       False, [RU(1 + i), "pm0", "pm2"], [PS(bd)])
                    mm(psum[:, bd, g * P:(g + 1) * P], ut_[:, i, g * P:(g + 1) * P], mprev[:, g * P:(g + 1) * P], False,
                       True, [RU(i), "pm1"], [PS(bd)])
                yield
                AT = ATb[i % 2]
                ar = "AT%d" % (i % 2)
                tt(AT, psum[:, bA, :].rearrange("p (h t) -> p h t", t=P), trim, ALU.mult,
                   [PS(bA)] + ["trim%d" % h for h in range(4)], [ar])
                rel(bA)
                act(dT, psum[:, bd, :].rearrange("p (g t) -> p g t", t=P), AF.Copy, [PS(bd)], ["dT"])
                rel(bd)
                yield
                bO = bank()
                for h in range(4):
                    o_ap = psum[:, bO, h * P:(h + 1) * P]
                    mm(o_ap, Vt_[:, i, h * P:(h + 1) * P], AT[:, h, :], True, first, [RV(i), ar], [PS(bO)])
                    if not first:
                        mm(o_ap, Sb[cur][:, h // 2, :], Qz_[:, h, seg], False, True, ["Sb%d" % cur, RQ(h)], [PS(bO)])
                by = bank()
                for g in range(4):
                    mm(psum[:, by, g * P:(g + 1) * P], w_pool_sb[:, g, :], dT[:, g, :], True, True, ["dT", "w_pool"],
                       [PS(by)])
            bs = bank()
            for h in range(4):
                c, po = h // 2, (h % 2) * 64
                mm(psum[po:po + 64, bs, c * P:(c + 1) * P], Ktt_[:, i, h * 64:(h + 1) * 64], Vt_[:, i, h * P:(h + 1) * P],
                   True, True, [RKT(i), RV(i)], [PS(bs)])
            yield
            for c in range(2):
                dcol = dec_[:, c, i:i + 1]
                if first:
                    ts(Sst[nxt][:, c, :], psum[:, bs, c * P:(c + 1) * P], dcol, None, ALU.mult, ALU.bypass,
                       [PS(bs), RD], ["S%d" % nxt])
                else:
                    ts(Sd[:, c, :], Sst[cur][:, c, :], dcol, None, ALU.mult, ALU.bypass, ["S%d" % cur, RD], ["Sd"])
                    stt(Sst[nxt][:, c, :], psum[:, bs, c * P:(c + 1) * P], dcol, Sd[:, c, :], ALU.mult, ALU.add,
                        [PS(bs), RD, "Sd"], ["S%d" % nxt])
            rel(bs)
            act(Sb[nxt], Sst[nxt], AF.Copy, ["S%d" % nxt], ["Sb%d" % nxt])
            state["cur"] = nxt
            state["first"] = False
            if not full:
                yield
                return
            ops_o = psum[:, bO, :].rearrange("p (h t) -> p h t", t=P)
            dma("sp", x1l[s1], x1b, xall[t * P:(t + 1) * P, :], [], [x1r])
            act(sq, ops_o, AF.Square, [PS(bO)], ["sq"])
            tt(yT[:, 0:4, seg], psum[:, by, :].rearrange("p (g t) -> p g t", t=P), psb, ALU.mult, [PS(by), "psb"], ["yTp%d" % i])
            rel(by)
            yield
            bb = bank()
            for h in range(4):
                mm(psum[:, bb, h * P:(h + 1) * P], ones_bf, sq[:, h, :], True, True, ["sq", "ones_bf"], [PS(bb)])
            yield
            act(rb, psum[:, bb, :].rearrange("p (h t) -> p h t", t=P), AF.Ln, [PS(bb)], ["rb"], bias=EPS, scale=1.0 / P)
            rel(bb)
            act(rb, rb, AF.Exp, ["rb"], ["rb"], scale=-0.5)
            yield
            tt(tb, ops_o, rb, ALU.mult, [PS(bO), "rb"], ["tb"])
            rel(bO)
            stt(yT[:, 4:8, seg], tb, c_ggla, sr_[:, :, seg], ALU.mult, ALU.mult,
                ["tb", "cols"] + [RS(h) for h in range(4)], ["yTg%d" % i])
            yield
            bm = bank_pair()
            for half in range(2):
                for k in range(KD):
                    mm(psum[:, bm + half, :], yT[:, k, seg], w_out_sb[:, k, half * 512:(half + 1) * 512], k == 0, k == KD - 1,
                       ["yTp%d" % i, "yTg%d" % i] + W_OUT, [PS(bm + half)])
            yield
            mixps = psum[:, bm:bm + 2, :].rearrange("p a n -> p (a n)")
            ss, ssr = statcol()
            act(m_t1, mixps, AF.Square, [PS(bm), PS(bm + 1)], ["m_t1", ssr], accum=ss)
            r, rres = rstd_from_ss(ss, ssr, 1.0 / D)
            tt(m_t1, mixps, gb_post_mix, ALU.mult, [PS(bm), PS(bm + 1), "gb_post_mix"], ["m_t1"])
            rel(bm, bm + 1)
            yield
            stt(x1b, m_t1, r, x1b, ALU.mult, ALU.add, ["m_t1", rres, x1r], [x1r])
            if t >= 16:
                dma("sp", x1l[s1], out[(t - 16) * P:(t - 15) * P, :], x1b, [x1r], ["out%d" % (t - 16)])
            else:
                ss2, ss2r = statcol()
                act(hx, x1b, AF.Square, [x1r], ["hx", ss2r], accum=ss2)
                r2, r2res = rstd_from_ss(ss2, ss2r, 1.0 / D)
                stt(hx, x1b, r2, gb_pre_ffn, ALU.mult, ALU.mult, [x1r, r2res, "gb_pre_ffn"], ["hx"])
                for k in range(KD):
                    tr(psT[:, k * P:(k + 1) * P], hx[:, k * P:(k + 1) * P], ["hx"], ["psT"])
                act(hh, psT.rearrange("p (k t) -> p k t", t=P)[:, :, P - 2:P], AF.Copy, ["psT"], ["hh"])
            if i == nt - 1 and tiles[-1] != NT_ALL - 1:
                S.add("dve", lambda e: e.tensor_copy(out=ut[1 - par][:, 0, :], in_=ut_[:, nt, :]), [RU(nt)],
                      ["ut0_%d" % (1 - par)])
            yield

        def b_gen():
            lag = 4 if full else 2
            gens = [chunk(i) for i in range(nt)]
            active = []
            nxt_i = 0
            while nxt_i < nt or active:
                if nxt_i < nt and len(active) < 3 and (not active or active[-1][1] >= lag):
                    active.append([gens[nxt_i], 0])
                    nxt_i += 1
                for a_ in list(active):
                    try:
                        next(a_[0])
                        a_[1] += 1
                    except StopIteration:
                        active.remove(a_)
                yield

        n_a2a = 1 + 6 + (4 if full else 2) + nt
        n_a2b = nt + ((nt + 4) if full else 0)
        n_b = (11 + 5 * (nt - 1)) if full else (3 * nt)
        return (prep_gen, 2 * nt + 3), (a2a_gen, n_a2a), (b_gen, n_b), (a2b_gen, n_a2b)

    step_no = [0]

    def run_merged(items, tags, scale=None):
        sid = step_no[0]
        step_no[0] += 1
        gens = [g() for g, _ in items]
        alive = [True] * len(gens)

        def advance(k):
            S.tag = tags[k]
            S.group = []
            try:
                next(gens[k])
                ok = True
            except StopIteration:
                alive[k] = False
                ok = False
            grp, S.group, S.tag = S.group, None, None
            return ok, grp

        if plan is not None:
            for k in plan[sid]:
                if alive[k]:
                    advance(k)
            for k in range(len(gens)):
                while alive[k]:
                    advance(k)
            return
        lens = [max(1, n) for _, n in items]
        if scale:
            lens = [l * scale.get(t, 1.0) for l, t in zip(lens, tags)]
        prog = [0] * len(gens)
        rec = [[] for _ in gens]
        while any(alive):
            k = min((j for j in range(len(gens)) if alive[j]), key=lambda j: (prog[j] + 0.5) / lens[j])
            ok, grp = advance(k)
            if ok:
                prog[k] += 1
                rec[k].append(grp)
        if record is not None:
            record.append((rec, list(tags)))

    def ffn_prep0(fence):
        pg = [prep_tile_gen(out[i * P:(i + 1) * P, :], f_xa, f_xs, gb_pre_ffn, "gb_pre_ffn", h2T, "h2T", i * P, "f",
                            extra=fence, rstd_fn=rstd_from_ss) for i in range(4)]
        live = []
        nx = 0
        while nx < 4 or live:
            if nx < 4 and len(live) < 2:
                live.append(pg[nx])
                nx += 1
            for g_ in list(live):
                try:
                    next(g_)
                except StopIteration:
                    live.remove(g_)
            yield

    for par in range(2):
        S.add("dve", lambda e, par=par: e.memset(ut[par][:, 0, :], 0.0), [], ["ut0_%d" % par])
        for h_ in range(4):
            rows = slice(64, 128) if h_ % 2 == 0 else slice(0, 64)
            S.add("dve", lambda e, par=par, h_=h_, rows=rows: e.memset(Qz[par][rows, h_, :], 0.0), [],
                  ["Qzz%d_%d" % (h_, par)])
    load_class(0)
    macros = []
    pref = list(range(0, HALO_T))
    if "no_prefix" not in opts:
        for m0 in range(0, len(pref), 4):
            macros.append((pref[m0:m0 + 4], False))
    for tl in ([15, 16, 17, 18], [19, 20, 21, 22], [23, 24, 25, 26], [27, 28, 29, 30], [31]):
        macros.append((tl, "no_full" not in opts))
    built = [make_macro(tl, fl, idx % 2) for idx, (tl, fl) in enumerate(macros)]
    nM = len(built)
    for t0_ in range(3):
        m_load(t0_)
    dma("sp", newlane("c"), gb_post_mix, gbs[1], [], ["gb_post_mix"])
    dma("sp", newlane("c"), gb_pre_ffn, gbs[2], [], ["gb_pre_ffn"])
    dma("sp", newlane("c"), gb_post_ffn, gbs[3], [], ["gb_post_ffn"])
    dma("sp", newlane("c"), psb, psb_d, [], ["psb"])
    for s in range(-2, nM):
        items, tags = [], []
        if 0 <= s + 2 < nM:
            items.append(built[s + 2][0])
            tags.append("prep")
        if 0 <= s + 1 < nM:
            items.append(built[s + 1][1])
            tags.append("a2")
            items.append(built[s + 1][3])
            tags.append("a2")
        if 0 <= s < nM:
            items.append(built[s][2])
            tags.append("b")
        if s == nM - 1 and not stop_after_mixer:
            assert (nM - 1) % 2 == 0
            items.append((lambda: ffn_prep0(fenceP), 12))
            tags.append("fprep")
        run_merged(items, tags, scale={"a2": 3.0} if s == nM - 2 else None)
        if s == nM - 2:
            fenceP = S.fence()
            fenceA = S.fence_tags(["prep", "a2"])
            load_class(1, extra=fenceA)
    fenceM = S.fence()

    def fblock_of(f):
        for bi, (f0, nf, g, v, o, _) in enumerate(fblk):
            if f0 <= f < f0 + nf:
                return bi, f - f0
        raise AssertionError

    X = fenceM
    def f_prep(m, i):
        prep_tile(out[(4 * m + i) * P:(4 * m + i + 1) * P, :], f_xa, f_xs, gb_pre_ffn, "gb_pre_ffn", h2T, "h2T", i * P,
                  "f", extra=X)

    load_class(2, extra=fenceM)
    for m in range(0 if stop_after_mixer else 4):
        halc, haln = hal[m % 2], hal[(m + 1) % 2]
        hcr, hnr = "hal%d" % (m % 2), "hal%d" % ((m + 1) % 2)
        bh = None
        if m == 0:
            bh = bank()
        for f in range(NF):
            bi, fi = fblock_of(f)
            _, _, gw, vw, ow, _ = fblk[bi]
            bgt, bvl = bank(), bank()
            for k in range(KD):
                mm(psum[:, bgt, :], gw[:, k, fi * P:(fi + 1) * P], h2T[:, k, :], k == 0, k == KD - 1,
                   ["h2T"] + FWG(bi), [PS(bgt)], extra=X, wfuse=True)
            if m == 0:
                for k in range(KD):
                    mm(psum[:, bh, 2 * f:2 * f + 2], gw[:, k, fi * P:(fi + 1) * P], hh[:, k, :], k == 0, k == KD - 1,
                       ["hh"] + FWG(bi), [PS(bh)], extra=X)
                act(halc[:, f, :], psum[:, bh, 2 * f:2 * f + 2], AF.Copy, [PS(bh)], [hcr], extra=X)
            for k in range(KD):
                mm(psum[:, bvl, :], vw[:, k, fi * P:(fi + 1) * P], h2T[:, k, :], k == 0, k == KD - 1,
                   ["h2T"] + FWV(bi), [PS(bvl)], extra=X, wfuse=True)
            acc, accr = accb[f % 2], "acc%d" % (f % 2)
            gl, glr = glb[f % 2], "gl%d" % (f % 2)
            gps = psum[:, bgt, :]
            act(acc, gps, AF.Identity, [PS(bgt), "cols"], [accr], bias=c_cb[:, f:f + 1], scale=c_cw[:, f, 2:3], extra=X)
            act(haln[:, f, :], gps[:, 510:512], AF.Copy, [PS(bgt)], [hnr], extra=X)
            stt(acc[:, 1:512], gps[:, 0:511], c_cw[:, f, 1:2], acc[:, 1:512], ALU.mult, ALU.add, [PS(bgt), "cols", accr],
                [accr], extra=X)
            stt(acc[:, 2:512], gps[:, 0:510], c_cw[:, f, 0:1], acc[:, 2:512], ALU.mult, ALU.add, [PS(bgt), "cols", accr],
                [accr], extra=X)
            rel(bgt)
            stt(acc[:, 0:1], halc[:, f, 1:2], c_cw[:, f, 1:2], acc[:, 0:1], ALU.mult, ALU.add, [hcr, "cols", accr],
                [accr], extra=X)
            stt(acc[:, 0:2], halc[:, f, 0:2], c_cw[:, f, 0:1], acc[:, 0:2], ALU.mult, ALU.add, [hcr, "cols", accr],
                [accr], extra=X)
            act(gl, acc, AF.Gelu, [accr], [glr], extra=X)
            tt(actb[:, f, :], gl, psum[:, bvl, :], ALU.mult, [glr, PS(bvl)], ["actb%d" % f], extra=X)
            rel(bvl)
        if bh is not None:
            rel(bh)
        def f_load(i_):
            s_ = (prep_cnt[0] + (i_ - f_load.base)) % 2
            dma("sp", xlf[s_], f_xa[s_], out[(4 * m + 4 + i_) * P:(4 * m + 5 + i_) * P, :], [], ["fxa%d" % s_], extra=X)

        if m < 3:
            f_load.base = 0
            f_load(0)
            f_load(1)
        for i in range(4):
            ti = 4 * m + i
            pgen = None
            if m < 3:
                pgen = prep_tile_gen(out[(4 * m + 4 + i) * P:(4 * m + 5 + i) * P, :], f_xa, f_xs, gb_pre_ffn, "gb_pre_ffn",
                                     h2T, "h2T", i * P, "f", extra=X, do_load=False)
                next(pgen)
                next(pgen)
                if i + 2 < 4:
                    f_load.base = i + 1
                    f_load(i + 2)
            bo_ = bank_pair()
            for f in range(NF):
                bi, fi = fblock_of(f)
                ow = fblk[bi][4]
                for half in range(2):
                    mm(psum[:, bo_ + half, :], actb[:, f, i * P:(i + 1) * P], ow[:, fi, half * 512:(half + 1) * 512],
                       f == 0, f == NF - 1, ["actb%d" % f, "fwo%d" % bi], [PS(bo_ + half)], extra=X)
            if pgen is not None:
                for _ in pgen:
                    pass
            ffps = psum[:, bo_:bo_ + 2, :].rearrange("p a n -> p (a n)")
            s1 = ti % 2
            x1b, x1r = f_x1[s1], "fx1%d" % s1
            dma("sp", x1l[s1], x1b, out[ti * P:(ti + 1) * P, :], ["out%d" % ti], [x1r], extra=X)
            ss, ssr = statcol()
            act(f_t1, ffps, AF.Square, [PS(bo_), PS(bo_ + 1)], ["f_t1", ssr], accum=ss, extra=X)
            r, rres = rstd_pool(ss, ssr, 1.0 / D, extra=X)
            tt(f_t1, ffps, gb_post_ffn, ALU.mult, [PS(bo_), PS(bo_ + 1), "gb_post_ffn"], ["f_t1"], extra=X)
            rel(bo_, bo_ + 1)
            stt(x1b, f_t1, r, x1b, ALU.mult, ALU.add, ["f_t1", rres, x1r], [x1r], extra=X)
            dma("sp", x1l[s1], out[ti * P:(ti + 1) * P, :], x1b, [x1r], ["out%d" % ti], extra=X)
    final_deps = S.fence()
    S.add("sp", lambda e: e.nop(), extra=final_deps)

    S.finalize()
    sems = {}
    for e in ("pe", "act", "dve", "pool"):
        sems[e] = es.enter_context(nc.semaphore("s_" + e))
    for lane in S.lane_cnt:
        sems[("L", lane)] = es.enter_context(nc.semaphore("l_" + lane))
    with nc.Block() as block:
        @block.tensor
        def _(e):
            S.emit_engine("pe", e, sems)

        @block.scalar
        def _(e):
            S.emit_engine("act", e, sems)

        @block.vector
        def _(e):
            S.emit_engine("dve", e, sems)

        @block.gpsimd
        def _(e):
            S.emit_engine("pool", e, sems)

        @block.sync
        def _(e):
            S.emit_engine("sp", e, sems)
    es.close()
    build_program.info = dict(M_BASE=M_BASE, A_END=A_END, M_USED=M_USED, gv_class=gv_class, wo_class=wo_class)
    return nc


def make_plan(record, slack=1.0, rng=None, noise=0.0):
    eng_free = {}
    finish = {}
    plan = []
    nsteps = len(record)
    owner = {}
    for sid_, (rec_, tags_) in enumerate(record):
        for k_, stream in enumerate(rec_):
            for grp in stream:
                for op in grp:
                    owner[op[0]] = (sid_, k_)

    op_eng = {}

    def dep_time(deps, sid, k, eng):
        t = 0.0
        for d in deps:
            o = owner.get(d)
            if o is not None and o[0] == sid and o[1] != k:
                continue
            if d in finish:
                t = max(t, finish[d] + (0.0 if op_eng.get(d) == eng else 0.8))
        return t

    def chain_cost(grp):
        per = {}
        for (idx, eng, cost, deps, is_dma) in grp:
            per[eng] = per.get(eng, 0.0) + (0.06 if is_dma else cost)
        return (max(per.values()) if per else 0.0) + 0.4

    for sid, (rec, tags) in enumerate(record):
        ptr = [0] * len(rec)
        order = []
        remaining = [sum(chain_cost(g) for g in r) for r in rec]
        boost = {k: (1e6 if (sid == nsteps - 2 and tags[k] == "a2") else 0.0) for k in range(len(rec))}
        while True:
            cands = [k for k in range(len(rec)) if ptr[k] < len(rec[k])]
            if not cands:
                break
            sts = {}
            for k in cands:
                grp = rec[k][ptr[k]]
                if not grp:
                    sts[k] = -1e9
                else:
                    idx, eng, cost, deps, is_dma = grp[0]
                    sts[k] = max(eng_free.get(eng, 0.0), dep_time(deps, sid, k, eng))
            tmin = min(sts.values())
            near = [k for k in cands if sts[k] <= tmin + slack]
            if rng is not None:
                k = max(near, key=lambda j: (remaining[j] * (1.0 + noise * (2.0 * rng.random() - 1.0)) + boost[j], -j))
            else:
                k = max(near, key=lambda j: (remaining[j] + boost[j], -j))
            for (idx, eng, cost, deps, is_dma) in rec[k][ptr[k]]:
                st = max(eng_free.get(eng, 0.0), dep_time(deps, sid, k, eng))
                op_eng[idx] = "dma" if is_dma else eng
                if is_dma:
                    eng_free[eng] = st + 0.06
                    finish[idx] = st + cost
                else:
                    eng_free[eng] = st + cost
                    finish[idx] = st + cost
            remaining[k] -= chain_cost(rec[k][ptr[k]])
            ptr[k] += 1
            order.append(k)
        plan.append(order)
        make_plan.step_ends = getattr(make_plan, "step_ends", [])
        if sid == 0:
            make_plan.step_ends = []
        make_plan.step_ends.append(dict(eng_free))
    make_plan.model_time = max(eng_free.values()) if eng_free else 0.0
    return plan


def search_plan(record, trials=60):
    import random
    best = make_plan(record)
    best_t = make_plan.model_time
    rng = random.Random(1234)
    for i in range(trials):
        sl = (0.5, 1.0, 1.5, 2.0)[i % 4]
        p = make_plan(record, slack=sl, rng=rng, noise=0.35)
        if make_plan.model_time < best_t:
            best, best_t = p, make_plan.model_time
    search_plan.model_time = best_t
    return best


def _pool_mats(first_half):
    wins = (2, 4, 8, 16)
    mats = np.zeros((3, 4, P, P), np.float32)
    j = np.arange(P)[:, None]
    i = np.arange(P)[None, :]
    for g, w in enumerate(wins):
        band = ((j <= i) & (j > i - w)).astype(np.float32)
        mats[0, g] = band / w - np.eye(P, dtype=np.float32)
        mats[1, g] = ((j + 0 > P + i - w)).astype(np.float32) / w
        if first_half:
            cnt = np.minimum(np.arange(1, P + 1), w).astype(np.float32)[None, :]
            mats[2, g] = band / cnt - np.eye(P, dtype=np.float32)
        else:
            mats[2, g] = mats[0, g]
    return mats


_NC_CACHE = {}


def kernel(x, g_pre_mix, w_in, w_pool, pool_scale, w_gate_up, b_gate, g_gla_norm, w_out,
           g_post_mix, g_pre_ffn, w_ffn_in, conv_w, conv_b, w_ffn_out, g_post_ffn):
    f = lambda a: np.ascontiguousarray(np.asarray(a, dtype=np.float32))
    x = f(x)
    B, SEQ, _ = x.shape
    gbs = np.stack([np.broadcast_to(f(g)[None, :], (P, D)) for g in (g_pre_mix, g_post_mix, g_pre_ffn, g_post_ffn)])
    cols = np.zeros((P, NCOL), np.float32)
    cols[:, 0:2] = f(b_gate).reshape(2, P).T
    cols[:, 2] = f(g_gla_norm)
    cols[:, 3:7] = f(pool_scale).reshape(4, P).T
    cw = f(conv_w)
    cols[:, 7:73] = cw.reshape(3, NF, P).transpose(2, 1, 0).reshape(P, NF * 3)
    cols[:, 73:95] = f(conv_b).reshape(NF, P).T
    cmask = np.zeros((P, 3, P), np.float32)
    cmask[:, 0, :] = np.eye(P)
    cmask[:, 1, :] = (np.arange(P)[:, None] <= np.arange(P)[None, :])
    cmask[:, 2, :] = 1.0
    psb = np.ascontiguousarray(np.broadcast_to(f(pool_scale).reshape(4, P).T[:, :, None], (P, 4, P)))
    shared = {
        "w_in": f(w_in), "w_out": f(w_out), "w_ffn_in": f(w_ffn_in), "w_ffn_out": f(w_ffn_out),
        "w_pool": f(w_pool), "w_gu": f(w_gate_up), "gbs": np.ascontiguousarray(gbs), "cols": cols,
        "cmask": cmask, "psb": psb,
    }
    pm = {True: _pool_mats(True), False: _pool_mats(False)}
    in_maps = []
    half = SEQ // 2
    for c in range(8):
        b, hf = c // 2, c % 2
        if hf == 0:
            xa = np.concatenate([np.zeros((half, D), np.float32), x[b, :half]], axis=0)
        else:
            xa = x[b]
        m = dict(shared)
        m["xall"] = np.ascontiguousarray(xa)
        m["pmats"] = pm[hf == 0]
        in_maps.append(m)
    if "nc" not in _NC_CACHE:
        rec = []
        build_program(record=rec)
        _NC_CACHE["nc"] = build_program(plan=search_plan(rec))
    res = run_bass_kernel_spmd(_NC_CACHE["nc"], in_maps, core_ids=list(range(8)))
    outp = np.empty((B, SEQ, D), np.float32)
    for c in range(8):
        b, hf = c // 2, c % 2
        outp[b, hf * half:(hf + 1) * half] = res.results[c]["out"]
    return outp
```
